# Optimizing a Trainium2 kernel written in Bass

```python
import math
import jax, jax.numpy as jnp
from jax import lax
import numpy as np

D_MODEL = 1024
BATCH = 4
SEQ = 8192
DEPTH = 2

HEAD_DIM = 64
Q_BLOCK = 128
NEG_INF = -1e30
LN_EPS = 1e-5
RMS_EPS = 1e-6
DN_ALPHA = (2.0 * DEPTH) ** 0.25
DN_BETA = (8.0 * DEPTH) ** -0.25
D_FF = 4 * D_MODEL

MLA_HEADS = (D_MODEL // 2) // HEAD_DIM
MLA_Q_RANK = D_MODEL // 4
MLA_KV_RANK = D_MODEL // 4
MLA_NOPE = 64
MLA_ROPE = 32
MLA_V = HEAD_DIM
ROPE_THETA = 10000.0

NSA_HEADS = (D_MODEL // 2) // HEAD_DIM
NSA_KV_GROUPS = 2
NSA_CMP_LEN = 32
NSA_CMP_STRIDE = 16
NSA_SEL_LEN = 64
NSA_SEL_TOPK = 16
NSA_WINDOW = 512
NSA_FORCE_BONUS = 1e3

DIFF_HEADS = D_MODEL // (2 * HEAD_DIM)
DIFF_D = HEAD_DIM

L0_SPLITS = (MLA_Q_RANK, MLA_KV_RANK, MLA_ROPE, NSA_HEADS * HEAD_DIM) + (NSA_KV_GROUPS * HEAD_DIM,) * 6 + (3 * NSA_HEADS,)
L0_IN_WIDTH = sum(L0_SPLITS)
L0_V_SEGMENTS = (5, 7, 9)
L0_MIX_WIDTH = MLA_HEADS * MLA_V + NSA_HEADS * HEAD_DIM

kernel_name = 'hybrid_mla_nsa_diffattn_deepnorm'


def layer_norm(x, g, b):
    xf = x.astype(jnp.float32)
    mu = jnp.mean(xf, axis=-1, keepdims=True)
    var = jnp.mean(jnp.square(xf - mu), axis=-1, keepdims=True)
    return ((xf - mu) * lax.rsqrt(var + LN_EPS) * g + b).astype(x.dtype)


def rms_norm(x, g):
    xf = x.astype(jnp.float32)
    return (xf * lax.rsqrt(jnp.mean(jnp.square(xf), axis=-1, keepdims=True) + RMS_EPS) * g).astype(x.dtype)


def rope_tables(seq):
    inv = 1.0 / (ROPE_THETA ** (jnp.arange(0, MLA_ROPE, 2, dtype=jnp.float32) / MLA_ROPE))
    ang = jnp.arange(seq, dtype=jnp.float32)[:, None] * inv[None, :]
    return jnp.cos(ang), jnp.sin(ang)


def apply_rope(x, cos, sin):
    x1, x2 = jnp.split(x.astype(jnp.float32), 2, axis=-1)
    return jnp.concatenate([x1 * cos - x2 * sin, x1 * sin + x2 * cos], axis=-1).astype(x.dtype)


def alibi_slopes(n):
    return jnp.asarray(2.0 ** (-8.0 * np.arange(1, n + 1) / n), dtype=jnp.float32)


def to_blocks(a):
    b, s = a.shape[:2]
    return jnp.moveaxis(a.reshape(b, s // Q_BLOCK, Q_BLOCK, *a.shape[2:]), 1, 0)


def from_blocks(a):
    a = jnp.moveaxis(a, 0, 1)
    return a.reshape(a.shape[0], a.shape[1] * a.shape[2], *a.shape[3:])


def block_starts(s):
    return jnp.arange(s // Q_BLOCK, dtype=jnp.int32) * Q_BLOCK


def causal_probs(q_blk, k, start, scale, slopes):
    sc = jnp.einsum('bqhd,bkhd->bhqk', q_blk, k, preferred_element_type=jnp.float32) * scale
    qpos = start + jnp.arange(Q_BLOCK)
    dist = (qpos[:, None] - jnp.arange(k.shape[1])[None, :]).astype(jnp.float32)
    if slopes is not None:
        sc = sc - slopes[None, :, None, None] * dist
    return jax.nn.softmax(jnp.where(dist >= 0, sc, NEG_INF), axis=-1)


def mla_core(q, k, v):
    scale = q.shape[-1] ** -0.5

    def block(args):
        qb, start = args
        p = causal_probs(qb, k, start, scale, None)
        return jnp.einsum('bhqk,bkhd->bqhd', p.astype(v.dtype), v)

    return from_blocks(lax.map(block, (to_blocks(q), block_starts(q.shape[1]))))


def compress_blocks(t, pos_emb, w1, w2):
    b, s, g, d = t.shape
    n_c = (s - NSA_CMP_LEN) // NSA_CMP_STRIDE + 1
    idx = jnp.arange(n_c)[:, None] * NSA_CMP_STRIDE + jnp.arange(NSA_CMP_LEN)[None, :]
    blk = t[:, idx] + pos_emb[:, None, :]
    flat = jnp.swapaxes(blk, 2, 3).reshape(b, n_c, g, NSA_CMP_LEN * d)
    return jax.nn.gelu(flat @ w1) @ w2


def cmp_to_sel_overlap(n_c, n_s):
    c0 = jnp.arange(n_c)[:, None] * NSA_CMP_STRIDE
    s0 = jnp.arange(n_s)[None, :] * NSA_SEL_LEN
    ov = jnp.minimum(c0 + NSA_CMP_LEN, s0 + NSA_SEL_LEN) - jnp.maximum(c0, s0)
    return jnp.maximum(ov, 0).astype(jnp.float32) / NSA_CMP_LEN


def nsa_core(q, k_c, v_c, k_s, v_s, k_w, v_w, gates):
    b, s, h, d = q.shape
    g, hg = NSA_KV_GROUPS, NSA_HEADS // NSA_KV_GROUPS
    n_c = k_c.shape[1]
    n_s = s // NSA_SEL_LEN
    top = min(NSA_SEL_TOPK, n_s)
    scale = d ** -0.5
    slopes = alibi_slopes(h).reshape(g, hg)[None, :, :, None, None]
    cmp_end = jnp.arange(n_c) * NSA_CMP_STRIDE + (NSA_CMP_LEN - 1)
    overlap = cmp_to_sel_overlap(n_c, n_s)
    k_blocks = jnp.transpose(k_s.reshape(b, n_s, NSA_SEL_LEN, g, d), (0, 3, 1, 2, 4))
    v_blocks = jnp.transpose(v_s.reshape(b, n_s, NSA_SEL_LEN, g, d), (0, 3, 1, 2, 4))
    pad = ((0, 0), (NSA_WINDOW, 0), (0, 0), (0, 0))
    k_wp = jnp.pad(k_w, pad)
    v_wp = jnp.pad(v_w, pad)
    bi = jnp.arange(b)[:, None, None, None]
    gi = jnp.arange(g)[None, :, None, None]
    sel_off = jnp.arange(NSA_SEL_LEN)
    blk_ids = jnp.arange(n_s)
    win_off = jnp.arange(NSA_WINDOW + Q_BLOCK)
    n_sel_keys = top * NSA_SEL_LEN

    def block(args):
        qb, gb, start = args
        qg = qb.reshape(b, Q_BLOCK, g, hg, d)
        qpos = start + jnp.arange(Q_BLOCK)
        dist = (qpos[:, None] - cmp_end[None, :]).astype(jnp.float32)
        valid = dist >= 0
        sc = jnp.einsum('bqghd,bcgd->bghqc', qg, k_c, preferred_element_type=jnp.float32) * scale
        sc = jnp.where(valid, sc - slopes * dist, NEG_INF)
        p_c = jnp.where(valid, jax.nn.softmax(sc, axis=-1), 0.0)
        o_c = jnp.einsum('bghqc,bcgd->bqghd', p_c.astype(v_c.dtype), v_c)
        imp = jnp.einsum('bghqc,cn->bgqn', p_c, overlap)
        cur = (qpos // NSA_SEL_LEN)[:, None]
        forced = (blk_ids == 0) | (blk_ids == cur) | (blk_ids == cur - 1)
        imp = jnp.where(blk_ids <= cur, imp + NSA_FORCE_BONUS * forced.astype(jnp.float32), NEG_INF)
        _, sel = lax.top_k(imp, top)
        ks = k_blocks[bi, gi, sel].reshape(b, g, Q_BLOCK, n_sel_keys, d)
        vs = v_blocks[bi, gi, sel].reshape(b, g, Q_BLOCK, n_sel_keys, d)
        kpos = (sel[..., None] * NSA_SEL_LEN + sel_off).reshape(b, g, Q_BLOCK, n_sel_keys)
        dist = (qpos[:, None] - kpos).astype(jnp.float32)[:, :, None]
        sc = jnp.einsum('bqghd,bgqkd->bghqk', qg, ks, preferred_element_type=jnp.float32) * scale
        sc = jnp.where(dist >= 0, sc - slopes * dist, NEG_INF)
        o_s = jnp.einsum('bghqk,bgqkd->bqghd', jax.nn.softmax(sc, axis=-1).astype(vs.dtype), vs)
        kw = lax.dynamic_slice_in_dim(k_wp, start, NSA_WINDOW + Q_BLOCK, axis=1)
        vw = lax.dynamic_slice_in_dim(v_wp, start, NSA_WINDOW + Q_BLOCK, axis=1)
        kpos_w = start - NSA_WINDOW + win_off
        dist_i = qpos[:, None] - kpos_w[None, :]
        valid = (dist_i >= 0) & (dist_i < NSA_WINDOW) & (kpos_w[None, :] >= 0)
        sc = jnp.einsum('bqghd,bkgd->bghqk', qg, kw, preferred_element_type=jnp.float32) * scale
        sc = jnp.where(valid, sc - slopes * dist_i.astype(jnp.float32), NEG_INF)
        o_w = jnp.einsum('bghqk,bkgd->bqghd', jax.nn.softmax(sc, axis=-1).astype(vw.dtype), vw)
        gg = gb.reshape(b, Q_BLOCK, g, hg, 3)
        o = gg[..., 0:1] * o_c + gg[..., 1:2] * o_s + gg[..., 2:3] * o_w
        return o.reshape(b, Q_BLOCK, h, d)

    return from_blocks(lax.map(block, (to_blocks(q), to_blocks(gates), block_starts(s))))


def mla_nsa_mixer(x, w_in, q_norm, w_uq, kv_norm, w_ukv, pos_k, w1_k, w2_k, pos_v, w1_v, w2_v, w_out, cos, sin):
    b, s, _ = x.shape
    h = x @ w_in
    parts = jnp.split(h, np.cumsum(L0_SPLITS)[:-1].tolist(), axis=-1)
    q_lat, kv_lat, k_rope, nq, nkc, nvc, nks, nvs, nkw, nvw, ng = parts
    q = (rms_norm(q_lat, q_norm) @ w_uq).reshape(b, s, MLA_HEADS, MLA_NOPE + MLA_ROPE)
    q_pe = apply_rope(q[..., MLA_NOPE:], cos[:, None, :], sin[:, None, :])
    kv = (rms_norm(kv_lat, kv_norm) @ w_ukv).reshape(b, s, MLA_HEADS, MLA_NOPE + MLA_V)
    k_pe = apply_rope(k_rope, cos, sin)
    q_full = jnp.concatenate([q[..., :MLA_NOPE], q_pe], axis=-1)
    k_full = jnp.concatenate([kv[..., :MLA_NOPE], jnp.broadcast_to(k_pe[:, :, None, :], (b, s, MLA_HEADS, MLA_ROPE))], axis=-1)
    o_mla = mla_core(q_full, k_full, kv[..., MLA_NOPE:])
    kv_shape = (b, s, NSA_KV_GROUPS, HEAD_DIM)
    k_c = compress_blocks(nkc.reshape(kv_shape), pos_k, w1_k, w2_k)
    v_c = compress_blocks(nvc.reshape(kv_shape), pos_v, w1_v, w2_v)
    gates = jax.nn.sigmoid(ng.reshape(b, s, NSA_HEADS, 3))
    o_nsa = nsa_core(nq.reshape(b, s, NSA_HEADS, HEAD_DIM), k_c, v_c, nks.reshape(kv_shape), nvs.reshape(kv_shape), nkw.reshape(kv_shape), nvw.reshape(kv_shape), gates)
    o = jnp.concatenate([o_mla.reshape(b, s, -1), o_nsa.reshape(b, s, -1)], axis=-1)
    return o @ w_out


def diff_mixer(x, w_qkv, lam_q1, lam_k1, lam_q2, lam_k2, subln_g, w_o, layer_idx):
    b, s, _ = x.shape
    q, k, v = jnp.split(x @ w_qkv, 3, axis=-1)
    q = q.reshape(b, s, DIFF_HEADS, 2, DIFF_D)
    k = k.reshape(b, s, DIFF_HEADS, 2, DIFF_D)
    v = v.reshape(b, s, DIFF_HEADS, 2 * DIFF_D)
    k1, k2 = k[..., 0, :], k[..., 1, :]
    lam_init = 0.8 - 0.6 * math.exp(-0.3 * layer_idx)
    lam = (jnp.exp(jnp.sum(lam_q1.astype(jnp.float32) * lam_k1.astype(jnp.float32)))
           - jnp.exp(jnp.sum(lam_q2.astype(jnp.float32) * lam_k2.astype(jnp.float32))) + lam_init)
    slopes = alibi_slopes(DIFF_HEADS)
    scale = DIFF_D ** -0.5

    def block(args):
        qb, start = args
        p1 = causal_probs(qb[..., 0, :], k1, start, scale, slopes)
        p2 = causal_probs(qb[..., 1, :], k2, start, scale, slopes)
        return jnp.einsum('bhqk,bkhe->bqhe', (p1 - lam * p2).astype(v.dtype), v)

    o = from_blocks(lax.map(block, (to_blocks(q), block_starts(s))))
    o = rms_norm(o, subln_g) * (1.0 - lam_init)
    return o.reshape(b, s, -1) @ w_o


def channel_mixer(x, w_up, w_down, ln_g, ln_b):
    y = jnp.square(jax.nn.relu(x @ w_up)) @ w_down
    return layer_norm(DN_ALPHA * x + y, ln_g, ln_b)


def _w(k, shape, fan_in, gain=1.0):
    return jax.random.normal(k, shape, jnp.float32) * (gain * fan_in ** -0.5)


def _gain(k, n):
    return 1.0 + 0.02 * jax.random.normal(k, (n,), jnp.float32)


def _bias(k, n):
    return 0.02 * jax.random.normal(k, (n,), jnp.float32)


def setup_inputs(seed: int = 0) -> dict:
    key = jax.random.key(seed)
    ks = iter(jax.random.split(key, 40))
    d = D_MODEL
    cmp_in = NSA_CMP_LEN * HEAD_DIM
    in_scale = jnp.concatenate([jnp.full((n,), DN_BETA if i in L0_V_SEGMENTS else 1.0, jnp.float32) for i, n in enumerate(L0_SPLITS)])
    ukv_scale = jnp.tile(jnp.concatenate([jnp.ones((MLA_NOPE,), jnp.float32), jnp.full((MLA_V,), DN_BETA, jnp.float32)]), MLA_HEADS)
    qkv_scale = jnp.concatenate([jnp.ones((2 * d,), jnp.float32), jnp.full((d,), DN_BETA, jnp.float32)])
    return {
        'x': jax.random.normal(next(ks), (BATCH, SEQ, d), jnp.float32),
        'l0_w_in': _w(next(ks), (d, L0_IN_WIDTH), d) * in_scale,
        'l0_mla_q_norm': _gain(next(ks), MLA_Q_RANK),
        'l0_mla_w_uq': _w(next(ks), (MLA_Q_RANK, MLA_HEADS * (MLA_NOPE + MLA_ROPE)), MLA_Q_RANK),
        'l0_mla_kv_norm': _gain(next(ks), MLA_KV_RANK),
        'l0_mla_w_ukv': _w(next(ks), (MLA_KV_RANK, MLA_HEADS * (MLA_NOPE + MLA_V)), MLA_KV_RANK) * ukv_scale,
        'l0_nsa_cmp_pos_k': 0.1 * jax.random.normal(next(ks), (NSA_CMP_LEN, HEAD_DIM), jnp.float32),
        'l0_nsa_cmp_w1_k': _w(next(ks), (cmp_in, HEAD_DIM), cmp_in),
        'l0_nsa_cmp_w2_k': _w(next(ks), (HEAD_DIM, HEAD_DIM), HEAD_DIM),
        'l0_nsa_cmp_pos_v': 0.1 * jax.random.normal(next(ks), (NSA_CMP_LEN, HEAD_DIM), jnp.float32),
        'l0_nsa_cmp_w1_v': _w(next(ks), (cmp_in, HEAD_DIM), cmp_in),
        'l0_nsa_cmp_w2_v': _w(next(ks), (HEAD_DIM, HEAD_DIM), HEAD_DIM),
        'l0_w_out': _w(next(ks), (L0_MIX_WIDTH, d), L0_MIX_WIDTH, DN_BETA),
        'l0_ln_mix_g': _gain(next(ks), d),
        'l0_ln_mix_b': _bias(next(ks), d),
        'l0_w_up': _w(next(ks), (d, D_FF), d, DN_BETA),
        'l0_w_down': _w(next(ks), (D_FF, d), D_FF, DN_BETA),
        'l0_ln_ffn_g': _gain(next(ks), d),
        'l0_ln_ffn_b': _bias(next(ks), d),
        'l1_w_qkv': _w(next(ks), (d, 3 * d), d) * qkv_scale,
        'l1_lam_q1': 0.1 * jax.random.normal(next(ks), (DIFF_D,), jnp.float32),
        'l1_lam_k1': 0.1 * jax.random.normal(next(ks), (DIFF_D,), jnp.float32),
        'l1_lam_q2': 0.1 * jax.random.normal(next(ks), (DIFF_D,), jnp.float32),
        'l1_lam_k2': 0.1 * jax.random.normal(next(ks), (DIFF_D,), jnp.float32),
        'l1_subln_g': _gain(next(ks), 2 * DIFF_D),
        'l1_w_o': _w(next(ks), (d, d), d, DN_BETA),
        'l1_ln_mix_g': _gain(next(ks), d),
        'l1_ln_mix_b': _bias(next(ks), d),
        'l1_w_up': _w(next(ks), (d, D_FF), d, DN_BETA),
        'l1_w_down': _w(next(ks), (D_FF, d), D_FF, DN_BETA),
        'l1_ln_ffn_g': _gain(next(ks), d),
        'l1_ln_ffn_b': _bias(next(ks), d),
    }


def reference(x, l0_w_in, l0_mla_q_norm, l0_mla_w_uq, l0_mla_kv_norm, l0_mla_w_ukv,
              l0_nsa_cmp_pos_k, l0_nsa_cmp_w1_k, l0_nsa_cmp_w2_k, l0_nsa_cmp_pos_v, l0_nsa_cmp_w1_v, l0_nsa_cmp_w2_v,
              l0_w_out, l0_ln_mix_g, l0_ln_mix_b, l0_w_up, l0_w_down, l0_ln_ffn_g, l0_ln_ffn_b,
              l1_w_qkv, l1_lam_q1, l1_lam_k1, l1_lam_q2, l1_lam_k2, l1_subln_g, l1_w_o,
              l1_ln_mix_g, l1_ln_mix_b, l1_w_up, l1_w_down, l1_ln_ffn_g, l1_ln_ffn_b):
    cos, sin = rope_tables(x.shape[1])
    mixer_params = [
        (l0_w_in, l0_mla_q_norm, l0_mla_w_uq, l0_mla_kv_norm, l0_mla_w_ukv, l0_nsa_cmp_pos_k, l0_nsa_cmp_w1_k,
         l0_nsa_cmp_w2_k, l0_nsa_cmp_pos_v, l0_nsa_cmp_w1_v, l0_nsa_cmp_w2_v, l0_w_out),
        (l1_w_qkv, l1_lam_q1, l1_lam_k1, l1_lam_q2, l1_lam_k2, l1_subln_g, l1_w_o),
    ]
    mix_norms = [(l0_ln_mix_g, l0_ln_mix_b), (l1_ln_mix_g, l1_ln_mix_b)]
    ffn_params = [(l0_w_up, l0_w_down, l0_ln_ffn_g, l0_ln_ffn_b), (l1_w_up, l1_w_down, l1_ln_ffn_g, l1_ln_ffn_b)]
    for i in range(DEPTH):
        if i % 2 == 0:
            y = mla_nsa_mixer(x, *mixer_params[i], cos, sin)
        else:
            y = diff_mixer(x, *mixer_params[i], i)
        x = layer_norm(DN_ALPHA * x + y, *mix_norms[i])
        x = channel_mixer(x, *ffn_params[i])
    return x
```

```python
import contextlib
import math
import numpy as np
import ml_dtypes
import concourse.bass as bass
import concourse.mybir as mybir
from concourse.bass_utils import run_bass_kernel_spmd

F32 = mybir.dt.float32
BF16 = mybir.dt.bfloat16
AF = mybir.ActivationFunctionType
ALU = mybir.AluOpType
PE, ACT, DVE, POOL, SP = "tensor", "scalar", "vector", "gpsimd", "sync"

D = 1024
HD = 64
NEG = -30000.0
LN_EPS = 1e-5
RMS_EPS = 1e-6
DN_ALPHA = 4.0 ** 0.25
QT = 512


class Prog:
    def __init__(self, nc):
        self.nc = nc
        self.ops = []
        self.last_writer = {}
        self.readers = {}
        self.last_dma = {}

    PSUM_PFX = ("PJ", "S0", "S1", "S2", "O0", "O1", "L", "MS", "Y", "TP", "H0", "H1")

    def op(self, eng, fn, reads=(), writes=(), dma_key=None, ndma=1):
        i = len(self.ops)
        ps = [k for k in reads if k.startswith(self.PSUM_PFX)]
        if ps:
            reads = [k for k in reads if k not in ps]
            writes = list(writes) + [k for k in ps if k not in writes]
        deps = set()
        for k in reads:
            w = self.last_writer.get(k)
            if w is not None:
                deps.add(w)
        for k in writes:
            w = self.last_writer.get(k)
            if w is not None:
                deps.add(w)
            for r in self.readers.get(k, ()):
                deps.add(r)
        if dma_key is not None:
            p = self.last_dma.get(dma_key)
            if p is not None:
                deps.add(p)
            self.last_dma[dma_key] = i
        deps.discard(i)
        self.ops.append(dict(i=i, eng=eng, fn=fn, deps=deps, dma_key=dma_key, ndma=ndma,
                             reads=tuple(reads), writes=tuple(writes)))
        for k in reads:
            self.readers.setdefault(k, []).append(i)
        for k in writes:
            self.last_writer[k] = i
            self.readers[k] = []
        return i

    def barrier(self):
        last = {}
        for o in self.ops:
            last[(o["eng"], o["dma_key"])] = o["i"]
        deps = set(last.values())
        for e in (PE, ACT, DVE, POOL, SP):
            i = len(self.ops)
            self.ops.append(dict(i=i, eng=e, fn=(lambda eo: None), deps=set(deps), dma_key=None, ndma=1,
                                 reads=("__bar__",), writes=("__bar__",)))

    def emit(self, final_wait_eng=SP):
        nc = self.nc
        ops = self.ops
        for o in ops:
            keep = set()
            for d in o["deps"]:
                p = ops[d]
                if p["dma_key"] is None and p["eng"] == o["eng"]:
                    if o["eng"] == PE or "__bar__" in o["reads"]:
                        continue
                keep.add(d)
            o["deps"] = keep
        needed = set()
        for o in ops:
            needed |= o["deps"]
        engs = [PE, ACT, DVE, POOL, SP]
        cnt = {e: 0 for e in engs}
        dcnt = {}
        for o in ops:
            if o["dma_key"] is not None:
                k = o["dma_key"]
                dcnt[k] = dcnt.get(k, 0) + 16 * o["ndma"]
                o["sig"] = ("d", k, dcnt[k])
            elif o["i"] in needed:
                cnt[o["eng"]] += 1
                o["sig"] = ("c", o["eng"], cnt[o["eng"]])
            else:
                o["sig"] = None
        with contextlib.ExitStack() as st:
            csem = {e: st.enter_context(nc.semaphore("c_" + e)) for e in engs}
            dsem = {k: st.enter_context(nc.semaphore("d_%d" % n)) for n, k in enumerate(dcnt)}
            block = st.enter_context(nc.Block())
            per_eng = {e: [o for o in ops if o["eng"] == e] for e in engs}

            def run(e, eobj):
                waited = {}
                for o in per_eng[e]:
                    tg = {}
                    for d in o["deps"]:
                        s = ops[d]["sig"]
                        key = (s[0], s[1])
                        tg[key] = max(tg.get(key, 0), s[2])
                    for key, v in tg.items():
                        if waited.get(key, 0) >= v:
                            continue
                        waited[key] = v
                        eobj.wait_ge(csem[key[1]] if key[0] == "c" else dsem[key[1]], v)
                    r = o["fn"](eobj)
                    s = o["sig"]
                    if s is None:
                        continue
                    if s[0] == "d":
                        rs = r if isinstance(r, (list, tuple)) else [r]
                        assert len(rs) == o["ndma"], (len(rs), o["ndma"])
                        for ins in rs:
                            ins.then_inc(dsem[s[1]], 16)
                    else:
                        ins = r[-1] if isinstance(r, (list, tuple)) else r
                        ins.then_inc(csem[s[1]], 1)
                if e == final_wait_eng:
                    for k, v in dcnt.items():
                        if waited.get(("d", k), 0) < v:
                            eobj.wait_ge(dsem[k], v)
                    for e2 in engs:
                        if e2 != e and cnt[e2] > 0 and waited.get(("c", e2), 0) < cnt[e2]:
                            eobj.wait_ge(csem[e2], cnt[e2])

            for e in engs:
                if not per_eng[e] and e != final_wait_eng:
                    continue
                getattr(block, e)(lambda eobj, e=e: run(e, eobj))
        return cnt, dcnt


class KB:
    def __init__(self, name="k"):
        self.nc = bass.Bass("TRN2", target_bir_lowering=False)
        self.P = Prog(self.nc)
        self.uid = 0
        self.dram = {}

    def din(self, name, shape, dt=F32):
        t = self.nc.dram_tensor(name, list(shape), dt, kind="ExternalInput").ap()
        self.dram[name] = t
        return t

    def dout(self, name, shape, dt=F32):
        t = self.nc.dram_tensor(name, list(shape), dt, kind="ExternalOutput").ap()
        self.dram[name] = t
        return t

    def dscr(self, name, shape, dt=BF16):
        return self.nc.dram_tensor(name, list(shape), dt, kind="Internal").ap()

    def sb(self, name, shape, dt=F32):
        return self.nc.alloc_sbuf_tensor(name, list(shape), dt)

    def ps(self, name, shape, dt=F32):
        return self.nc.alloc_psum_tensor(name, list(shape), dt)

    def key(self, pfx="k"):
        self.uid += 1
        return "%s_%d" % (pfx, self.uid)

    def load(self, dst_ap, src_ap, wkey, dma_key=None, eng=SP, reads=()):
        dk = dma_key or self.key("ld")
        return self.P.op(eng, lambda e: e.dma_start(out=dst_ap, in_=src_ap), reads=reads, writes=[wkey], dma_key=dk)

    def load_cast(self, dst_sb_ap, src_dram_ap, wkey, dma_key=None):
        return self.load(dst_sb_ap, src_dram_ap, wkey, dma_key, eng=POOL)


def _alibi_slopes(n):
    return (2.0 ** (-8.0 * np.arange(1, n + 1) / n)).astype(np.float32)


class AttnPipe:
    def __init__(self, kb, n_s=3, n_p=4):
        self.kb = kb
        self.S = [kb.ps("Sbank%d" % i, [128, QT], F32) for i in range(n_s)]
        self.Pt = [kb.sb("Pt%d" % i, [128, QT], BF16) for i in range(n_p)]
        self.si = 0
        self.pi = 0

    def run(self, items, lookahead=2):
        P = self.kb.P
        n = len(items)
        slots = [None] * n
        for step in range(n + lookahead):
            if step < n:
                it = items[step]
                s = self.si % len(self.S)
                self.si += 1
                slots[step] = s
                Sb = self.S[s]

                def qk(e, it=it, Sb=Sb):
                    mm = it["qk"]
                    r = None
                    for idx, (l, rr, a, b) in enumerate(mm):
                        r = e.matmul(Sb[:, a:b], l, rr, start=(idx == 0), stop=(idx == len(mm) - 1))
                    return r
                P.op(PE, qk, reads=it["qk_reads"], writes=["S%d" % s])
            t = step - lookahead
            if t >= 0:
                it = items[t]
                s = slots[t]
                Sb = self.S[s]
                p = self.pi % len(self.Pt)
                self.pi += 1
                Pb = self.Pt[p]
                c0, c1 = it["c0"], it["c1"]
                P.op(ACT, lambda e, it=it, Sb=Sb, Pb=Pb, c0=c0, c1=c1: e.activation(
                    out=Pb[:, c0:c1], in_=Sb[:, c0:c1], func=AF.Exp, scale=it["scale"], bias=it["bias"]),
                    reads=["S%d" % s] + list(it.get("exp_reads", ())), writes=["P%d" % p])
                P.op(PE, lambda e, it=it, Pb=Pb, c0=c0, c1=c1: it["pv"](e, Pb, c0, c1),
                     reads=["P%d" % p] + list(it["pv_reads"]), writes=list(it["pv_writes"]))
                if it.get("after") is not None:
                    it["after"]()


def _qal_rows(slopes, scale):
    qi = np.arange(QT)
    out = np.zeros((len(slopes), 3, QT), np.float32)
    for i, s in enumerate(slopes):
        c = float(s) / scale
        out[i, 0] = -c * (qi % 256)
        out[i, 1] = -c * (qi - qi % 256)
        out[i, 2] = c
    return out


def _consts_common():
    ki = np.arange(128)
    c = {}
    c["ident"] = np.eye(128, dtype=np.float32)
    c["negtri"] = np.where(ki[None, :] >= ki[:, None], 0.0, NEG).astype(np.float32)
    c["neganti"] = np.where(ki[None, :] < ki[:, None], 0.0, NEG).astype(np.float32)
    return c


def _copy(P, eng, dst, src, reads, writes):
    if eng == ACT:
        return P.op(ACT, lambda e: e.copy(out=dst, in_=src), reads=reads, writes=writes)
    return P.op(eng, lambda e: e.tensor_copy(out=dst, in_=src), reads=reads, writes=writes)


def _mm_group(e, out, pairs, **kw):
    r = None
    n = len(pairs)
    for i, (l, rr) in enumerate(pairs):
        r = e.matmul(out, l, rr, start=(i == 0), stop=(i == n - 1), **kw)
    return r


def build_diff(S, lam_init):
    kb = KB()
    P = kb.P
    NT = S // QT
    NKT = S // 128
    scale = HD ** -0.5
    xT = kb.din("xT", [D, S], BF16)
    wq = kb.din("wq", [D, 512])
    wk = kb.din("wk", [D, 512])
    wv = kb.din("wv", [D, 512])
    lamv = kb.din("lamv", [1, 256])
    gsub = kb.din("gsub", [128, 1])
    qal = kb.din("qal", [4, 3, QT])
    kal = kb.din("kal", [3, S])
    ident = kb.din("ident", [128, 128])
    negtri = kb.din("negtri", [128, 128])
    btabd = kb.din("btab", [128, 4 * 67])
    oT = kb.dout("oT", [512, S], BF16)
    btab = kb.sb("btab_sb", [128, 4 * 67], F32)
    kb.load(btab[:], btabd, "btab", "c0")

    wq_sb = kb.sb("wq_sb", [128, 8, 512], BF16)
    wk_sb = kb.sb("wk_sb", [128, 8, 512], BF16)
    wv_sb = kb.sb("wv_sb", [128, 8, 512], BF16)
    ident_b = kb.sb("ident_b", [128, 128], BF16)
    negtri_b = kb.sb("negtri_b", [128, 128], BF16)
    ones_b = kb.sb("ones_b", [128, 128], BF16)
    ones_f = kb.sb("ones_f", [128, 128], F32)
    onesdiv_b = kb.sb("onesdiv_b", [128, 128], BF16)
    eps_c = kb.sb("eps_c", [128, 1], F32)
    lv = kb.sb("lv", [1, 256], F32)
    lt = kb.sb("lt", [1, 136], F32)
    neglam = kb.sb("neglam", [128, 1], F32)
    gsc = kb.sb("gsc", [128, 1], F32)
    KT = kb.sb("KT", [67, 4, S], BF16)
    Vc = kb.sb("Vc", [128, NKT, 256], BF16)
    QTb = [kb.sb("QTb%d" % i, [67, 4, QT], BF16) for i in range(2)]
    xt = [kb.sb("xt%d" % i, [128, 8, QT], BF16) for i in range(2)]
    o1s = kb.sb("o1s", [128, QT], F32)
    o2s = kb.sb("o2s", [128, QT], F32)
    lr = kb.sb("lr", [33, QT], F32)
    sq = kb.sb("sq", [128, QT], BF16)
    rstd = kb.sb("rstd", [128, QT], F32)
    ost = [kb.sb("ost%d" % i, [128, QT], BF16) for i in range(2)]
    pipe = AttnPipe(kb)
    O = [kb.ps("O%d" % i, [128, QT], F32) for i in range(2)]
    L = kb.ps("L", [128, QT], F32)
    PJ = [kb.ps("PJ%d" % i, [128, QT], F32) for i in range(2)]

    xTv = xT.rearrange("(c p) t -> p c t", p=128)
    for nm, src, dst in (("wq", wq, wq_sb), ("wk", wk, wk_sb), ("wv", wv, wv_sb)):
        kb.load_cast(dst[:], src.rearrange("(c p) n -> p c n", p=128), nm, "wc")
    kb.load_cast(ident_b[:], ident, "ident", "wc")
    kb.load_cast(negtri_b[:], negtri, "negtri", "wc")
    for hm in range(4):
        kb.load_cast(KT[64:67, hm, :], kal, "KTal", "wc")
    kb.load(lv[:], lamv, "lv", "c0")
    kb.load(gsc[:], gsub, "gsc0", "c0")
    P.op(DVE, lambda e: e.memset(ones_b[:], 1.0), writes=["ones_b"])
    P.op(DVE, lambda e: e.memset(ones_f[:], 1.0), writes=["ones_f"])
    P.op(DVE, lambda e: e.memset(onesdiv_b[:], 1.0 / 128.0), writes=["onesdiv_b"])
    P.op(DVE, lambda e: e.memset(eps_c[:], RMS_EPS), writes=["eps_c"])
    P.op(DVE, lambda e: e.tensor_tensor(out=lt[:, 0:64], in0=lv[:, 0:64], in1=lv[:, 64:128], op=ALU.mult), reads=["lv"], writes=["lt_a"])
    P.op(DVE, lambda e: e.tensor_tensor(out=lt[:, 64:128], in0=lv[:, 128:192], in1=lv[:, 192:256], op=ALU.mult), reads=["lv"], writes=["lt_b"])
    P.op(DVE, lambda e: e.tensor_reduce(out=lt[:, 128:129], in_=lt[:, 0:64], axis=mybir.AxisListType.X, op=ALU.add), reads=["lt_a"], writes=["lt_s1"])
    P.op(DVE, lambda e: e.tensor_reduce(out=lt[:, 129:130], in_=lt[:, 64:128], axis=mybir.AxisListType.X, op=ALU.add), reads=["lt_b"], writes=["lt_s2"])
    P.op(ACT, lambda e: e.activation(out=lt[:, 130:132], in_=lt[:, 128:130], func=AF.Exp), reads=["lt_s1", "lt_s2"], writes=["lt_e"])
    P.op(DVE, lambda e: e.tensor_tensor(out=lt[:, 132:133], in0=lt[:, 131:132], in1=lt[:, 130:131], op=ALU.subtract), reads=["lt_e"], writes=["lt_d"])
    P.op(DVE, lambda e: e.tensor_scalar(out=lt[:, 133:134], in0=lt[:, 132:133], scalar1=-float(lam_init), scalar2=None, op0=ALU.add), reads=["lt_d"], writes=["lt_nl"])
    P.op(PE, lambda e: e.matmul(PJ[0][:, 0:1], ones_f[0:1, :], lt[0:1, 133:134], start=True, stop=True), reads=["ones_f", "lt_nl"], writes=["PJ0"])
    P.op(DVE, lambda e: e.tensor_copy(out=neglam[:], in_=PJ[0][:, 0:1]), reads=["PJ0"], writes=["neglam"])
    P.op(DVE, lambda e: e.tensor_scalar(out=gsc[:], in0=gsc[:], scalar1=float(1.0 - lam_init), scalar2=None, op0=ALU.mult), reads=["gsc0"], writes=["gsc"])

    pjn = [0]

    def pj():
        pjn[0] += 1
        return pjn[0] % 2

    cpn = [0]

    def cp_eng():
        cpn[0] += 1
        return DVE if cpn[0] % 2 else ACT

    ostn = [0]
    for p in range(2):
        for hl in range(2):
            for st in range(2):
                for m in range(2):
                    kb.load_cast(QTb[st][64:67, hl * 2 + m, :], qal[2 * p + hl], "QTal%d" % st, "wc")
        for j in range(NT):
            st = j % 2
            c0 = j * QT
            X = xt[st]
            kb.load(X[:], xTv[:, :, c0:c0 + QT], "xt%d" % st, "xt%d" % st)
            for hm in range(4):
                gcol = ((2 * p + hm // 2) * 2 + hm % 2) * 64
                for which, w_sb, wname in (("q", wq_sb, "wq"), ("k", wk_sb, "wk")):
                    b = pj()
                    P.op(PE, lambda e, b=b, w_sb=w_sb, gcol=gcol, X=X: _mm_group(
                        e, PJ[b][0:64, :], [(w_sb[:, k, gcol:gcol + 64], X[:, k, :]) for k in range(8)]),
                        reads=[wname, "xt%d" % st], writes=["PJ%d" % b])
                    if which == "q":
                        _copy(P, cp_eng(), QTb[st][0:64, hm, :], PJ[b][0:64, :], ["PJ%d" % b], ["QT%d_%d" % (st, hm)])
                    else:
                        _copy(P, cp_eng(), KT[0:64, hm, c0:c0 + QT], PJ[b][0:64, :], ["PJ%d" % b], ["KT_%d_%d" % (hm, j)])
            for u in range(4):
                b = pj()
                P.op(PE, lambda e, b=b, u=u, X=X, p=p: _mm_group(
                    e, PJ[b][:, 0:256], [(X[:, k, u * 128:(u + 1) * 128], wv_sb[:, k, p * 256:(p + 1) * 256]) for k in range(8)]),
                    reads=["wv", "xt%d" % st], writes=["PJ%d" % b])
                _copy(P, cp_eng(), Vc[:, 4 * j + u, :], PJ[b][:, 0:256], ["PJ%d" % b], ["V_%d" % (4 * j + u)])
            items = []
            for hl in range(2):
                for m in range(2):
                    hm = hl * 2 + m
                    nt = 4 * j + 4
                    for t in range(nt):
                        d = t - 4 * j
                        a = 128 * d if d > 0 else 0
                        qk = [(KT[0:67, hm, t * 128:(t + 1) * 128], QTb[st][0:67, hm, a:QT], a, QT)]
                        rd = ["KT_%d_%d" % (hm, t // 4), "KTal", "QT%d_%d" % (st, hm), "QTal%d" % st]
                        if d >= 0:
                            qk.append((ident_b[:, :], negtri_b[:, :], a, a + 128))
                            rd += ["ident", "negtri"]

                        def pv(e, Pb, a_, b_, hl=hl, m=m, t=t, nt=nt):
                            e.matmul(O[m][:, a_:b_], Vc[:, t, hl * 128:(hl + 1) * 128], Pb[:, a_:b_],
                                     start=(t == 0), stop=(t == nt - 1), skip_group_check=True)
                            return e.matmul(L[32 * m:32 * m + 1, a_:b_], ones_b[:, 0:1], Pb[:, a_:b_],
                                            start=(t == 0), stop=(t == nt - 1), skip_group_check=True)
                        bcol = (2 * p + hl) * 67 + (4 * j - t) + 3
                        it = dict(qk=qk, qk_reads=rd, c0=a, c1=QT, scale=scale, bias=btab[:, bcol:bcol + 1], exp_reads=["btab"],
                                  pv=pv, pv_reads=["V_%d" % t, "ones_b"], pv_writes=["O%d" % m, "L"])
                        items.append(it)
                def fin(hl=hl, j=j, c0=c0, p=p):
                    P.op(ACT, lambda e: e.copy(out=o1s[:], in_=O[0][:]), reads=["O0"], writes=["o1s"])
                    P.op(ACT, lambda e: e.copy(out=o2s[:], in_=O[1][:]), reads=["O1"], writes=["o2s"])
                    P.op(DVE, lambda e: e.tensor_scalar(out=lr[0:1, :], in0=L[0:1, :], scalar1=1e-30, scalar2=None, op0=ALU.max), reads=["L"], writes=["lr0"])
                    P.op(DVE, lambda e: e.tensor_scalar(out=lr[32:33, :], in0=L[32:33, :], scalar1=1e-30, scalar2=None, op0=ALU.max), reads=["L"], writes=["lr1"])
                    P.op(DVE, lambda e: e.reciprocal(out=lr[0:1, :], in_=lr[0:1, :]), reads=["lr0"], writes=["lr0"])
                    P.op(DVE, lambda e: e.reciprocal(out=lr[32:33, :], in_=lr[32:33, :]), reads=["lr1"], writes=["lr1"])
                    P.op(DVE, lambda e: e.tensor_scalar(out=lr[32:33, :], in0=lr[32:33, :], scalar1=neglam[32:33, 0:1], scalar2=None, op0=ALU.mult), reads=["lr1", "neglam"], writes=["lr1"])
                    P.op(PE, lambda e: e.matmul(PJ[0][:, :], ones_f[0:1, :], lr[0:1, :], start=True, stop=True), reads=["ones_f", "lr0"], writes=["PJ0"])
                    P.op(PE, lambda e: e.matmul(PJ[1][:, :], ones_f[32:33, :], lr[32:33, :], start=True, stop=True), reads=["ones_f", "lr1"], writes=["PJ1"])
                    P.op(DVE, lambda e: e.tensor_tensor(out=o1s[:], in0=o1s[:], in1=PJ[0][:, :], op=ALU.mult), reads=["o1s", "PJ0"], writes=["o1s"])
                    P.op(DVE, lambda e: e.tensor_tensor(out=o2s[:], in0=o2s[:], in1=PJ[1][:, :], op=ALU.mult), reads=["o2s", "PJ1"], writes=["o2s"])
                    P.op(POOL, lambda e: e.tensor_tensor(out=o1s[:], in0=o1s[:], in1=o2s[:], op=ALU.add), reads=["o1s", "o2s"], writes=["o1s"])
                    P.op(ACT, lambda e: e.activation(out=sq[:], in_=o1s[:], func=AF.Square), reads=["o1s"], writes=["sq"])
                    P.op(PE, lambda e: e.matmul(PJ[0][:, :], onesdiv_b[:, :], sq[:, :], start=True, stop=True), reads=["onesdiv_b", "sq"], writes=["PJ0"])
                    P.op(ACT, lambda e: e.activation(out=rstd[:], in_=PJ[0][:, :], func=AF.Ln, bias=eps_c[:, 0:1]), reads=["PJ0", "eps_c"], writes=["rstd"])
                    P.op(ACT, lambda e: e.activation(out=rstd[:], in_=rstd[:], func=AF.Exp, scale=-0.5), reads=["rstd"], writes=["rstd"])
                    P.op(DVE, lambda e: e.tensor_tensor(out=o1s[:], in0=o1s[:], in1=rstd[:], op=ALU.mult), reads=["o1s", "rstd"], writes=["o1s"])
                    ostn[0] += 1
                    ob = ostn[0] % 2
                    P.op(POOL, lambda e: e.tensor_scalar(out=ost[ob][:], in0=o1s[:], scalar1=gsc[:, 0:1], scalar2=None, op0=ALU.mult), reads=["o1s", "gsc"], writes=["ost%d" % ob])
                    gh = 2 * p + hl
                    P.op(SP, lambda e: e.dma_start(out=oT[gh * 128:(gh + 1) * 128, c0:c0 + QT], in_=ost[ob][:]), reads=["ost%d" % ob], dma_key="ost%d" % ob)
                items[-1]["after"] = fin
            pipe.run(items)
    P.emit()
    return kb


def build_post(T, emit_T):
    kb = KB()
    P = kb.P
    NT = T // QT
    oT = kb.din("oT", [D, T], BF16)
    x = kb.din("x", [T, D])
    w_out = kb.din("w_out", [D, D])
    w_up = kb.din("w_up", [D, 4 * D])
    w_down = kb.din("w_down", [4 * D, D])
    ln = kb.din("ln", [4, D])
    ident = kb.din("ident", [128, 128])
    y = kb.dout("y", [T, D])
    if emit_T:
        yT = kb.dout("yT", [D, T], BF16)
    wup_b = kb.dscr("wup_b", [D, 4 * D])
    wdn_b = kb.dscr("wdn_b", [4 * D, D])

    wo_sb = kb.sb("wo_sb", [128, 8, D], BF16)
    wu = [kb.sb("wu%d" % i, [128, 8, D], BF16) for i in range(2)]
    wd = [kb.sb("wd%d" % i, [128, 8, D], BF16) for i in range(2)]
    lnb = kb.sb("lnb", [128, 4, D], F32)
    ident_f = kb.sb("ident_f", [128, 128], F32)
    eps_c = kb.sb("eps_c", [128, 1], F32)
    oTt = kb.sb("oTt", [128, 8, QT], BF16)
    xa = kb.sb("xa", [128, 4, D], F32)
    z = kb.sb("z", [128, 4, D], F32)
    x1T = kb.sb("x1T", [128, 8, QT], BF16)
    h2 = [kb.sb("h2_%d" % i, [128, 8, QT], BF16) for i in range(2)]
    rr = [kb.sb("rr%d" % i, [128, QT], F32) for i in range(2)]
    stt = kb.sb("stt", [128, 2, 6], F32)
    mv = kb.sb("mv", [128, 4], F32)
    Y = kb.ps("Y", [128, D], F32)
    TP = kb.ps("TP", [128, 8, 128], F32)
    H = [kb.ps("H%d" % i, [128, QT], F32) for i in range(2)]
    Y2 = kb.ps("Y2", [128, D], F32)

    for c in range(8):
        P.op(POOL, lambda e, c=c: e.dma_start(out=wup_b[c * 128:(c + 1) * 128, :], in_=w_up[c * 128:(c + 1) * 128, :]),
             writes=["wup_b"], dma_key="wcast")
    for c in range(8):
        P.op(POOL, lambda e, c=c: e.dma_start(out=wdn_b[c * 512:(c + 1) * 512, :], in_=w_down[c * 512:(c + 1) * 512, :]),
             writes=["wdn_b"], dma_key="wcast")
    kb.load_cast(wo_sb[:], w_out.rearrange("(c p) n -> p c n", p=128), "wo", "wcast")
    kb.load(lnb[:], ln.partition_broadcast(128), "lnb", "c0")
    kb.load(ident_f[:], ident, "ident", "c0")
    P.op(DVE, lambda e: e.memset(eps_c[:], LN_EPS), writes=["eps_c"])
    oTv = oT.rearrange("(c p) t -> p c t", p=128)
    wupv = wup_b.rearrange("(c p) n -> p c n", p=128)
    wdnv = wdn_b.rearrange("(f p) n -> p f n", p=128)
    gi = [0]

    def layer_norm(buf, u, gcol, bcol, rkeys, wkey):
        P.op(DVE, lambda e: e.bn_stats(out=stt[:, 0, :], in_=buf[:, u, 0:512]), reads=rkeys, writes=["stt0"])
        P.op(DVE, lambda e: e.bn_stats(out=stt[:, 1, :], in_=buf[:, u, 512:1024]), reads=rkeys, writes=["stt1"])
        P.op(DVE, lambda e: e.bn_aggr(out=mv[:, 0:2], in_=stt[:, :, :]), reads=["stt0", "stt1"], writes=["mv"])
        P.op(ACT, lambda e: e.activation(out=mv[:, 2:3], in_=mv[:, 1:2], func=AF.Ln, bias=eps_c[:, 0:1]), reads=["mv", "eps_c"], writes=["mv2"])
        P.op(ACT, lambda e: e.activation(out=mv[:, 3:4], in_=mv[:, 2:3], func=AF.Exp, scale=-0.5), reads=["mv2"], writes=["mv3"])
        P.op(DVE, lambda e: e.tensor_scalar(out=buf[:, u, :], in0=buf[:, u, :], scalar1=mv[:, 0:1], scalar2=mv[:, 3:4],
                                            op0=ALU.subtract, op1=ALU.mult), reads=rkeys + ["mv", "mv3"], writes=[wkey])
        P.op(POOL, lambda e: e.tensor_tensor(out=buf[:, u, :], in0=buf[:, u, :], in1=lnb[:, gcol, :], op=ALU.mult), reads=[wkey, "lnb"], writes=[wkey])
        P.op(POOL, lambda e: e.tensor_tensor(out=buf[:, u, :], in0=buf[:, u, :], in1=lnb[:, bcol, :], op=ALU.add), reads=[wkey, "lnb"], writes=[wkey])

    def transpose_to(buf, u, dstT, rkeys, wkey):
        def tp(e):
            r = None
            for k in range(8):
                r = e.transpose(TP[:, k, :], buf[:, u, k * 128:(k + 1) * 128], ident_f[:, :])
            return r
        P.op(PE, tp, reads=rkeys + ["ident"], writes=["TP"])
        P.op(ACT, lambda e: e.copy(out=dstT[:, :, u * 128:(u + 1) * 128], in_=TP[:, :, :]), reads=["TP"], writes=[wkey])

    for i in range(NT):
        c0 = i * QT
        kb.load(oTt[:], oTv[:, :, c0:c0 + QT], "oTt", "oTt", reads=["oTt_free"])
        kb.load(xa[:], x[c0:c0 + QT, :].rearrange("(u p) d -> p u d", p=128), "xa_all", "xa")
        for u in range(4):
            def mmy(e, u=u):
                r = None
                for hf in range(2):
                    r = _mm_group(e, Y[:, hf * 512:(hf + 1) * 512],
                                  [(oTt[:, k, u * 128:(u + 1) * 128], wo_sb[:, k, hf * 512:(hf + 1) * 512]) for k in range(8)])
                return r
            P.op(PE, mmy, reads=["oTt", "wo"], writes=["Y"])
            P.op(DVE, lambda e, u=u: e.scalar_tensor_tensor(out=z[:, u, :], in0=xa[:, u, :], scalar=float(DN_ALPHA), in1=Y[:, :],
                                                            op0=ALU.mult, op1=ALU.add), reads=["xa_all", "Y"], writes=["z%d" % u])
            layer_norm(z, u, 0, 1, ["z%d" % u], "z%d" % u)
            transpose_to(z, u, x1T, ["z%d" % u], "x1T_%d" % u)
        for g in range(4):
            s2 = gi[0] % 2
            gi[0] += 1
            kb.load(wu[s2][:], wupv[:, :, g * D:(g + 1) * D], "wu%d" % s2, "wu%d" % s2, reads=["wup_b"])
            kb.load(wd[s2][:], wdnv[:, g * 8:(g + 1) * 8, :], "wd%d" % s2, "wd%d" % s2, reads=["wdn_b"])
            hb = h2[s2]
            for f in range(8):
                b = f % 2
                P.op(PE, lambda e, b=b, f=f, s2=s2: _mm_group(e, H[b][:, :], [(wu[s2][:, k, f * 128:(f + 1) * 128], x1T[:, k, :]) for k in range(8)]),
                     reads=["wu%d" % s2] + ["x1T_%d" % u for u in range(4)], writes=["H%d" % b])
                P.op(ACT, lambda e, b=b: e.activation(out=rr[b][:], in_=H[b][:, :], func=AF.Relu), reads=["H%d" % b], writes=["rr%d" % b])
                P.op(POOL, lambda e, b=b, f=f, hb=hb: e.tensor_tensor(out=hb[:, f, :], in0=rr[b][:], in1=rr[b][:], op=ALU.mult),
                     reads=["rr%d" % b], writes=["h2_%d_%d" % (s2, f)])
            for u in range(4):
                def mmd(e, u=u, hb=hb, s2=s2):
                    r = None
                    for hf in range(2):
                        r = _mm_group(e, Y2[:, hf * 512:(hf + 1) * 512],
                                      [(hb[:, f, u * 128:(u + 1) * 128], wd[s2][:, f, hf * 512:(hf + 1) * 512]) for f in range(8)])
                    return r
                P.op(PE, mmd, reads=["wd%d" % s2] + ["h2_%d_%d" % (s2, f) for f in range(8)], writes=["Y2"])
                if g == 0:
                    P.op(DVE, lambda e, u=u: e.scalar_tensor_tensor(out=xa[:, u, :], in0=z[:, u, :], scalar=float(DN_ALPHA), in1=Y2[:, :],
                                                                    op0=ALU.mult, op1=ALU.add), reads=["z%d" % u, "Y2", "xa_all"], writes=["acc%d" % u])
                else:
                    P.op(DVE, lambda e, u=u: e.tensor_tensor(out=xa[:, u, :], in0=xa[:, u, :], in1=Y2[:, :], op=ALU.add),
                         reads=["acc%d" % u, "Y2"], writes=["acc%d" % u])
        for u in range(4):
            layer_norm(xa, u, 2, 3, ["acc%d" % u], "acc%d" % u)
            P.op(SP, lambda e, u=u, c0=c0: e.dma_start(out=y[c0 + u * 128:c0 + (u + 1) * 128, :], in_=xa[:, u, :]),
                 reads=["acc%d" % u], writes=["xa_all"] if u == 3 else [], dma_key="yout%d" % (u % 2))
            if emit_T:
                transpose_to(xa, u, oTt, ["acc%d" % u], "yTs_%d" % u)
        if emit_T:
            P.op(SP, lambda e, c0=c0: e.dma_start(out=yT.rearrange("(c p) t -> p c t", p=128)[:, :, c0:c0 + QT], in_=oTt[:]),
                 reads=["yTs_%d" % u for u in range(4)], writes=["oTt_free"], dma_key="yTout")
    P.emit()
    return kb


def build_mla(S):
    kb = KB()
    P = kb.P
    NT = S // QT
    NKT = S // 128
    scale = 96 ** -0.5
    xT = kb.din("xT", [D, S])
    wa = kb.din("wa", [D, 704])
    gq = kb.din("gq", [128, 2])
    gkv = kb.din("gkv", [128, 2])
    wuq = kb.din("wuq", [256, 768])
    wuk = kb.din("wuk", [256, 256])
    wuv = kb.din("wuv", [256, 256])
    CCd = kb.din("CC", [32, S])
    SSd = kb.din("SS", [32, S])
    ident = kb.din("ident", [128, 128])
    negtri = kb.din("negtri", [128, 128])
    oT = kb.dout("oT", [256, S], BF16)

    wa_sb = kb.sb("wa_sb", [128, 8, 704], BF16)
    wuq_sb = kb.sb("wuq_sb", [128, 2, 768], BF16)
    wuk_sb = kb.sb("wuk_sb", [128, 2, 256], BF16)
    wuv_sb = kb.sb("wuv_sb", [128, 2, 256], BF16)
    wtmp = kb.sb("wtmp", [128, 2, 768], F32)
    g_sb = kb.sb("g_sb", [128, 4], F32)
    ident_b = kb.sb("ident_b", [128, 128], BF16)
    negtri_b = kb.sb("negtri_b", [128, 128], BF16)
    ones_f = kb.sb("ones_f", [128, 128], F32)
    onesdiv_b = kb.sb("onesdiv_b", [128, 128], BF16)
    eps_c = kb.sb("eps_c", [128, 1], F32)
    KT = kb.sb("KT", [96, 4, S], BF16)
    Vm = kb.sb("Vm", [128, NKT, 4, 65], BF16)
    QTm = [kb.sb("QTm%d" % i, [96, 4, QT], BF16) for i in range(2)]
    xt = [kb.sb("xt%d" % i, [128, 8, QT], BF16) for i in range(2)]
    tab = kb.sb("tab", [96, 2, QT], F32)
    sqb = kb.sb("sqb", [128, 4, QT], BF16)
    latf = kb.sb("latf", [128, 4, QT], F32)
    latn = kb.sb("latn", [128, 4, QT], BF16)
    rstd = kb.sb("rstd", [128, 2, QT], F32)
    t1 = kb.sb("t1", [96, QT], F32)
    t2 = kb.sb("t2", [96, QT], F32)
    Ob = kb.sb("Ob", [65, QT], F32)
    rl = kb.sb("rl", [65, QT], F32)
    ost = [kb.sb("ost%d" % i, [64, QT], BF16) for i in range(2)]
    pipe = AttnPipe(kb)
    O = [kb.ps("O%d" % i, [128, QT], F32) for i in range(2)]
    PJ = [kb.ps("PJ%d" % i, [128, QT], F32) for i in range(3)]

    xTv = xT.rearrange("(c p) t -> p c t", p=128)
    kb.load_cast(wa_sb[:], wa.rearrange("(c p) n -> p c n", p=128), "wa", "wc")
    kb.load_cast(ident_b[:], ident, "ident", "wc")
    kb.load_cast(negtri_b[:], negtri, "negtri", "wc")
    kb.load(g_sb[:, 0:2], gq, "g_q", "c0")
    kb.load(g_sb[:, 2:4], gkv, "g_kv", "c0")
    P.op(DVE, lambda e: e.memset(ones_f[:], 1.0), writes=["ones_f"])
    P.op(DVE, lambda e: e.memset(onesdiv_b[:], 1.0 / 256.0), writes=["onesdiv_b"])
    P.op(DVE, lambda e: e.memset(eps_c[:], RMS_EPS), writes=["eps_c"])
    P.op(POOL, lambda e: e.memset(Vm[:, :, :, 64:65], 1.0), writes=["Vm_ones"])
    for nm, src, dst, n, gc in (("wuq", wuq, wuq_sb, 768, 0), ("wuk", wuk, wuk_sb, 256, 2), ("wuv", wuv, wuv_sb, 256, 2)):
        kb.load(wtmp[:, :, 0:n], src.rearrange("(c p) n -> p c n", p=128), "wtmp", "c0")
        for c in range(2):
            P.op(DVE, lambda e, dst=dst, n=n, c=c, gc=gc: e.tensor_scalar(out=dst[:, c, :], in0=wtmp[:, c, 0:n], scalar1=g_sb[:, gc + c:gc + c + 1],
                                                                      scalar2=None, op0=ALU.mult), reads=["wtmp", "g_q", "g_kv"], writes=[nm])

    pjn = [0]

    def pj():
        pjn[0] += 1
        return pjn[0] % 3

    cpn = [0]

    def cp_eng():
        cpn[0] += 1
        return DVE if cpn[0] % 2 else ACT
    ostn = [0]

    def rope(psA, psB, dst_ap, rkeys, wkey):
        P.op(DVE, lambda e: e.tensor_tensor(out=t1[64:96, :], in0=psA, in1=tab[64:96, 0, :], op=ALU.mult), reads=rkeys[0:1] + ["tab"], writes=["t1"])
        P.op(DVE, lambda e: e.tensor_tensor(out=t2[64:96, :], in0=psB, in1=tab[64:96, 1, :], op=ALU.mult), reads=rkeys[1:2] + ["tab"], writes=["t2"])
        P.op(POOL, lambda e: e.tensor_tensor(out=dst_ap, in0=t1[64:96, :], in1=t2[64:96, :], op=ALU.add), reads=["t1", "t2"], writes=[wkey])

    for j in range(NT):
        st = j % 2
        c0 = j * QT
        X = xt[st]
        kb.load_cast(X[:], xTv[:, :, c0:c0 + QT], "xt%d" % st, "xt%d" % st)
        P.op(SP, lambda e, c0=c0: [e.dma_start(out=tab[64:96, 0, :], in_=CCd[:, c0:c0 + QT]), e.dma_start(out=tab[64:96, 1, :], in_=SSd[:, c0:c0 + QT])],
             writes=["tab"], dma_key="tab", ndma=2)
        for c in range(4):
            b = pj()
            P.op(PE, lambda e, b=b, c=c, X=X: _mm_group(e, PJ[b][:, :], [(wa_sb[:, k, c * 128:(c + 1) * 128], X[:, k, :]) for k in range(8)]),
                 reads=["wa", "xt%d" % st], writes=["PJ%d" % b])
            P.op(ACT, lambda e, b=b, c=c: e.activation(out=sqb[:, c, :], in_=PJ[b][:, :], func=AF.Square), reads=["PJ%d" % b], writes=["sqb%d" % c])
            P.op(DVE, lambda e, b=b, c=c: e.tensor_copy(out=latf[:, c, :], in_=PJ[b][:, :]), reads=["PJ%d" % b], writes=["latf%d" % c])
        for w in range(2):
            b = pj()
            P.op(PE, lambda e, b=b, w=w: _mm_group(e, PJ[b][:, :], [(onesdiv_b[:, :], sqb[:, 2 * w + c, :]) for c in range(2)]),
                 reads=["onesdiv_b", "sqb%d" % (2 * w), "sqb%d" % (2 * w + 1)], writes=["PJ%d" % b])
            P.op(ACT, lambda e, b=b, w=w: e.activation(out=rstd[:, w, :], in_=PJ[b][:, :], func=AF.Ln, bias=eps_c[:, 0:1]), reads=["PJ%d" % b, "eps_c"], writes=["rstd%d" % w])
            P.op(ACT, lambda e, w=w: e.activation(out=rstd[:, w, :], in_=rstd[:, w, :], func=AF.Exp, scale=-0.5), reads=["rstd%d" % w], writes=["rstd%d" % w])
            for c in range(2):
                cc = 2 * w + c
                P.op(DVE if c else POOL, lambda e, cc=cc, w=w: e.tensor_tensor(out=latn[:, cc, :], in0=latf[:, cc, :], in1=rstd[:, w, :], op=ALU.mult),
                     reads=["latf%d" % cc, "rstd%d" % w], writes=["latn%d" % cc])
        bA, bB = pj(), pj()
        P.op(PE, lambda e, bA=bA, X=X: _mm_group(e, PJ[bA][0:96, :], [(wa_sb[:, k, 512:608], X[:, k, :]) for k in range(8)]), reads=["wa", "xt%d" % st], writes=["PJ%d" % bA])
        P.op(PE, lambda e, bB=bB, X=X: _mm_group(e, PJ[bB][0:96, :], [(wa_sb[:, k, 608:704], X[:, k, :]) for k in range(8)]), reads=["wa", "xt%d" % st], writes=["PJ%d" % bB])
        rope(PJ[bA][64:96, :], PJ[bB][64:96, :], KT[64:96, 0, c0:c0 + QT], ["PJ%d" % bA, "PJ%d" % bB], "KTpe0_%d" % j)
        for hl in range(1, 4):
            P.op(POOL if hl % 2 else DVE, lambda e, hl=hl, c0=c0: e.tensor_copy(out=KT[64:96, hl, c0:c0 + QT], in_=KT[64:96, 0, c0:c0 + QT]),
                 reads=["KTpe0_%d" % j], writes=["KTpe%d_%d" % (hl, j)])
        for hl in range(4):
            bA, bB = pj(), pj()
            P.op(PE, lambda e, bA=bA, hl=hl: _mm_group(e, PJ[bA][0:96, :], [(wuq_sb[:, c, hl * 192:hl * 192 + 96], latn[:, c, :]) for c in range(2)]),
                 reads=["wuq", "latn0", "latn1"], writes=["PJ%d" % bA])
            P.op(PE, lambda e, bB=bB, hl=hl: _mm_group(e, PJ[bB][0:96, :], [(wuq_sb[:, c, hl * 192 + 96:hl * 192 + 192], latn[:, c, :]) for c in range(2)]),
                 reads=["wuq", "latn0", "latn1"], writes=["PJ%d" % bB])
            rope(PJ[bA][64:96, :], PJ[bB][64:96, :], QTm[st][64:96, hl, :], ["PJ%d" % bA, "PJ%d" % bB], "QTpe%d_%d" % (st, hl))
            _copy(P, ACT, QTm[st][0:64, hl, :], PJ[bA][0:64, :], ["PJ%d" % bA], ["QTn%d_%d" % (st, hl)])
            b = pj()
            P.op(PE, lambda e, b=b, hl=hl: _mm_group(e, PJ[b][0:64, :], [(wuk_sb[:, c, hl * 64:(hl + 1) * 64], latn[:, 2 + c, :]) for c in range(2)]),
                 reads=["wuk", "latn2", "latn3"], writes=["PJ%d" % b])
            _copy(P, cp_eng(), KT[0:64, hl, c0:c0 + QT], PJ[b][0:64, :], ["PJ%d" % b], ["KTn%d_%d" % (hl, j)])
        for u in range(4):
            b = pj()
            P.op(PE, lambda e, b=b, u=u: _mm_group(e, PJ[b][:, 0:256], [(latn[:, 2 + c, u * 128:(u + 1) * 128], wuv_sb[:, c, :]) for c in range(2)]),
                 reads=["wuv", "latn2", "latn3"], writes=["PJ%d" % b])
            _copy(P, cp_eng(), Vm[:, 4 * j + u, :, 0:64], PJ[b][:, 0:256].rearrange("p (h d) -> p h d", h=4), ["PJ%d" % b], ["V_%d" % (4 * j + u)])
        items = []
        for hl in range(4):
            ob = hl % 2
            nt = 4 * j + 4
            for t in range(nt):
                d = t - 4 * j
                a = 128 * d if d > 0 else 0
                qk = [(KT[0:96, hl, t * 128:(t + 1) * 128], QTm[st][0:96, hl, a:QT], a, QT)]
                rd = ["KTn%d_%d" % (hl, t // 4), "KTpe%d_%d" % (hl, t // 4), "QTpe%d_%d" % (st, hl), "QTn%d_%d" % (st, hl)]
                if d >= 0:
                    qk.append((ident_b[:, :], negtri_b[:, :], a, a + 128))
                    rd += ["ident", "negtri"]

                def pv(e, Pb, a_, b_, hl=hl, t=t, nt=nt, ob=ob):
                    return e.matmul(O[ob][0:65, a_:b_], Vm[:, t, hl, :], Pb[:, a_:b_], start=(t == 0), stop=(t == nt - 1), skip_group_check=True)
                items.append(dict(qk=qk, qk_reads=rd, c0=a, c1=QT, scale=scale, bias=0.0, pv=pv,
                                  pv_reads=["V_%d" % t, "Vm_ones"], pv_writes=["O%d" % ob]))

            def fin(hl=hl, ob=ob, c0=c0):
                P.op(ACT, lambda e: e.copy(out=Ob[:], in_=O[ob][0:65, :]), reads=["O%d" % ob], writes=["Ob"])
                P.op(DVE, lambda e: e.tensor_scalar(out=rl[64:65, :], in0=Ob[64:65, :], scalar1=1e-30, scalar2=None, op0=ALU.max), reads=["Ob"], writes=["rl"])
                P.op(DVE, lambda e: e.reciprocal(out=rl[64:65, :], in_=rl[64:65, :]), reads=["rl"], writes=["rl"])
                b = pj()
                P.op(PE, lambda e: e.matmul(PJ[b][0:64, :], ones_f[64:65, 0:64], rl[64:65, :], start=True, stop=True), reads=["ones_f", "rl"], writes=["PJ%d" % b])
                ostn[0] += 1
                os_ = ostn[0] % 2
                P.op(DVE, lambda e: e.tensor_tensor(out=ost[os_][:], in0=Ob[0:64, :], in1=PJ[b][0:64, :], op=ALU.mult), reads=["Ob", "PJ%d" % b], writes=["ost%d" % os_])
                P.op(SP, lambda e: e.dma_start(out=oT[hl * 64:(hl + 1) * 64, c0:c0 + QT], in_=ost[os_][:]), reads=["ost%d" % os_], dma_key="ost%d" % os_)
            items[-1]["after"] = fin
        pipe.run(items)
    P.emit()
    return kb


def _rope_tabs(S):
    inv = (1.0 / (np.float32(10000.0) ** (np.arange(0, 32, 2, dtype=np.float32) / np.float32(32)))).astype(np.float32)
    ang = np.arange(S, dtype=np.float32)[:, None] * inv[None, :]
    cos, sin = np.cos(ang).astype(np.float32), np.sin(ang).astype(np.float32)
    CC = np.concatenate([cos.T, cos.T], 0)
    SS = np.concatenate([-sin.T, sin.T], 0)
    return np.ascontiguousarray(CC), np.ascontiguousarray(SS)


def prep_mla(inp, j, S):
    w_in = np.asarray(inp['l0_w_in'])
    kr = w_in[:, 512:544]
    z64 = np.zeros((D, 64), np.float32)
    wa = np.concatenate([w_in[:, 0:512], z64, kr, z64, kr[:, 16:32], kr[:, 0:16]], 1)
    wuq_f = np.asarray(inp['l0_mla_w_uq']).reshape(256, 8, 96)[:, 4 * j:4 * j + 4]
    wuq = np.concatenate([wuq_f, wuq_f[..., 0:64], wuq_f[..., 80:96], wuq_f[..., 64:80]], -1).reshape(256, 768)
    wukv = np.asarray(inp['l0_mla_w_ukv']).reshape(256, 8, 128)[:, 4 * j:4 * j + 4]
    wuk = np.ascontiguousarray(wukv[..., 0:64]).reshape(256, 256)
    wuv = wukv[..., 64:128].reshape(256, 256)
    CC, SS = _rope_tabs(S)
    cc = _consts_common()
    d = dict(wa=wa, gq=np.asarray(inp['l0_mla_q_norm']).reshape(2, 128).T, gkv=np.asarray(inp['l0_mla_kv_norm']).reshape(2, 128).T,
             wuq=wuq, wuk=wuk, wuv=wuv, CC=CC, SS=SS, ident=cc['ident'], negtri=cc['negtri'])
    return {k: np.ascontiguousarray(v, dtype=np.float32) for k, v in d.items()}


def build_nsa(S):
    kb = KB()
    P = kb.P
    NT = S // QT
    NKT = S // 128
    NSLOT = S // 16
    NCH = (NSLOT + 127) // 128
    scale = HD ** -0.5
    xT = kb.din("xT", [D, S])
    wb = kb.din("wb", [D, 652])
    W1d = kb.din("W1bd", [128, 32 * 128])
    W2d = kb.din("W2bd", [128, 128])
    posTd = kb.din("posT", [128, 32])
    qal = kb.din("qal", [4, 3, QT])
    kal = kb.din("kal", [3, S])
    kalc = kb.din("kalc", [3, NCH * 128])
    ident = kb.din("ident", [128, 128])
    negtri = kb.din("negtri", [128, 128])
    neganti = kb.din("neganti", [128, 128])
    cmaskd = kb.din("cmask", [128, 5 * QT])
    Ed = kb.din("E", [128, S])
    OVd = kb.din("OV", [128, NCH * 128])
    Ttd = kb.din("Tt", [128, 256])
    gseld = kb.din("gsel", [12, 12 * 64])
    btabd = kb.din("btab", [128, 4 * 67 + 4 * 16])
    btab = kb.sb("btab_sb", [128, 4 * 67 + 4 * 16], F32)
    kb.load(btab[:], btabd, "btab", "c0")
    oT = kb.dout("oT", [256, S], BF16)

    wb_sb = kb.sb("wb_sb", [128, 8, 652], BF16)
    W1 = kb.sb("W1", [128, 32, 128], BF16)
    W2 = kb.sb("W2", [128, 128], BF16)
    posT = kb.sb("posT_b", [128, 32], BF16)
    cbias = kb.sb("cbias", [128, 1], F32)
    ident_b = kb.sb("ident_b", [128, 128], BF16)
    negtri_b = kb.sb("negtri_b", [128, 128], BF16)
    neganti_b = kb.sb("neganti_b", [128, 128], BF16)
    cmask = kb.sb("cmask_b", [128, 5, QT], BF16)
    E_sb = kb.sb("E_sb", [128, S], BF16)
    OV_sb = kb.sb("OV_sb", [128, NCH, 128], BF16)
    Tt = kb.sb("Tt_sb", [128, 256], F32)
    gsel = kb.sb("gsel_sb", [12, 12, 64], F32)
    ones_f = kb.sb("ones_f", [128, 128], F32)
    KT2 = kb.sb("KT2", [67, 2, S], BF16)
    V2 = kb.sb("V2", [128, NKT, 2, 65], BF16)
    KcT = kb.sb("KcT", [67, NCH * 128], BF16)
    Vcm = kb.sb("Vcm", [128, NCH, 65], BF16)
    geluT = kb.sb("geluT", [128, 128], BF16)
    kcraw = kb.sb("kcraw", [128, 16 + QT], BF16)
    QTn = [kb.sb("QTn%d" % i, [67, 4, QT], BF16) for i in range(2)]
    xt = [kb.sb("xt%d" % i, [128, 8, QT], BF16) for i in range(2)]
    gT = kb.sb("gT", [12, QT], F32)
    zt = kb.sb("zt", [128, 4, 32], F32)
    NST = kb.sb("NST", [128, QT], BF16)
    negsel = kb.sb("negsel", [128, 4, 128], BF16)
    impacc = kb.sb("impacc", [128, 4, 128], F32)
    impw = kb.sb("impw", [128, 2, 128], F32)
    m8 = kb.sb("m8", [128, 16], F32)
    lsum = kb.sb("lsum", [128, 4], F32)
    ObC = kb.sb("ObC", [65, 4, QT], F32)
    Ob3 = kb.sb("Ob3", [65, 3, QT], F32)
    rl3 = kb.sb("rl3", [65, 3, QT], F32)
    tmpc = kb.sb("tmpc", [64, 3, QT], F32)
    ost = [kb.sb("ost%d" % i, [64, QT], BF16) for i in range(2)]
    pipe = AttnPipe(kb)
    O = [kb.ps("O%d" % i, [128, QT], F32) for i in range(2)]
    PJ = [kb.ps("PJ%d" % i, [128, QT], F32) for i in range(2)]
    MS = kb.ps("MS", [128, 4, 128], F32)

    xTv = xT.rearrange("(c p) t -> p c t", p=128)
    kb.load_cast(wb_sb[:], wb.rearrange("(c p) n -> p c n", p=128), "wb", "wc")
    kb.load_cast(W1[:], W1d.rearrange("p (l o) -> p l o", l=32), "W1", "wc")
    kb.load_cast(W2[:], W2d, "W2", "wc")
    kb.load_cast(posT[:], posTd, "posT", "wc")
    kb.load_cast(ident_b[:], ident, "ident", "wc")
    kb.load_cast(negtri_b[:], negtri, "negtri", "wc")
    kb.load_cast(neganti_b[:], neganti, "neganti", "wc")
    kb.load_cast(cmask[:], cmaskd.rearrange("p (m q) -> p m q", m=5), "cmask", "wc")
    kb.load_cast(E_sb[:], Ed, "E", "wc")
    kb.load_cast(OV_sb[:], OVd.rearrange("p (c n) -> p c n", n=128), "OV", "wc")
    kb.load(Tt[:], Ttd, "Tt", "c0")
    kb.load(gsel[:], gseld.rearrange("p (a d) -> p a d", d=64), "gsel", "c0")
    for i in range(2):
        kb.load_cast(KT2[64:67, i, :], kal, "KTal", "wc")
    kb.load_cast(KcT[64:67, :], kalc, "KcTal", "wc")
    for st in range(2):
        for hl in range(4):
            kb.load_cast(QTn[st][64:67, hl, :], qal[hl], "QTal%d" % st, "wc")
    P.op(DVE, lambda e: e.memset(ones_f[:], 1.0), writes=["ones_f"])
    P.op(POOL, lambda e: e.memset(V2[:, :, :, 64:65], 1.0), writes=["V2_ones"])
    P.op(POOL, lambda e: e.memset(Vcm[:, :, 64:65], 1.0), writes=["Vcm_ones"])
    P.op(POOL, lambda e: e.memset(Vcm[:, :, 0:64], 0.0), writes=["Vcm_all"])
    P.op(POOL, lambda e: e.memset(KcT[0:64, :], 0.0), writes=["KcT_all"])
    P.op(POOL, lambda e: e.memset(kcraw[:, 0:16], 0.0), writes=["kcraw_h"])
    P.op(PE, lambda e: _mm_group(e, PJ[0][:, 0:1], [(W1[:, l, :], posT[:, l:l + 1]) for l in range(32)]), reads=["W1", "posT"], writes=["PJ0"])
    P.op(DVE, lambda e: e.tensor_copy(out=cbias[:], in_=PJ[0][:, 0:1]), reads=["PJ0"], writes=["cbias"])

    pjn = [0]

    def pj():
        pjn[0] += 1
        return pjn[0] % 2
    cpn = [0]

    def cp_eng():
        cpn[0] += 1
        return DVE if cpn[0] % 2 else ACT
    ostn = [0]
    obn = [0]

    for j in range(NT):
        st = j % 2
        c0 = j * QT
        q0 = c0
        X = xt[st]
        xk = "xt%d" % st
        kb.load_cast(X[:], xTv[:, :, c0:c0 + QT], xk, xk)

        def proj(lo, hi, dst, wkeys, eng=None, X=X, xk=xk):
            b = pj()
            P.op(PE, lambda e: _mm_group(e, PJ[b][0:hi - lo, :], [(wb_sb[:, k, lo:hi], X[:, k, :]) for k in range(8)]), reads=["wb", xk], writes=["PJ%d" % b])
            _copy(P, eng or cp_eng(), dst, PJ[b][0:hi - lo, :], ["PJ%d" % b], wkeys)
        for hl in range(4):
            proj(hl * 64, hl * 64 + 64, QTn[st][0:64, hl, :], ["QT%d_%d" % (st, hl)])
        if j > 0:
            P.op(POOL, lambda e: e.tensor_copy(out=kcraw[:, 0:16], in_=kcraw[:, QT:QT + 16]), reads=["kcraw_m"], writes=["kcraw_h"])
        proj(256, 384, kcraw[:, 16:16 + QT], ["kcraw_m"])
        proj(384, 448, KT2[0:64, 0, c0:c0 + QT], ["KTs_%d" % j])
        proj(448, 512, KT2[0:64, 1, c0:c0 + QT], ["KTw_%d" % j])
        for u in range(4):
            b = pj()
            P.op(PE, lambda e, b=b, u=u, X=X: _mm_group(e, PJ[b][:, 0:128], [(X[:, k, u * 128:(u + 1) * 128], wb_sb[:, k, 512:640]) for k in range(8)]),
                 reads=["wb", xk], writes=["PJ%d" % b])
            _copy(P, cp_eng(), V2[:, 4 * j + u, :, 0:64], PJ[b][:, 0:128].rearrange("p (h d) -> p h d", h=2), ["PJ%d" % b], ["V2_%d" % (4 * j + u)])
        b = pj()
        P.op(PE, lambda e, b=b, X=X: _mm_group(e, PJ[b][0:12, :], [(wb_sb[:, k, 640:652], X[:, k, :]) for k in range(8)]), reads=["wb", xk], writes=["PJ%d" % b])
        P.op(ACT, lambda e, b=b: e.activation(out=gT[:], in_=PJ[b][0:12, :], func=AF.Exp, scale=-1.0), reads=["PJ%d" % b], writes=["gT"])
        P.op(DVE, lambda e: e.tensor_scalar(out=gT[:], in0=gT[:], scalar1=1.0, scalar2=None, op0=ALU.add), reads=["gT"], writes=["gT"])
        P.op(DVE, lambda e: e.reciprocal(out=gT[:], in_=gT[:]), reads=["gT"], writes=["gT"])
        jc = j % 4
        tcj = j // 4
        if jc == 0:
            P.op(POOL, lambda e: e.memset(geluT[:], 0.0), writes=["geluT"])
        b = pj()
        P.op(PE, lambda e, b=b: _mm_group(e, PJ[b][:, 0:32], [(W1[:, l, :], kcraw[:, l:l + 497:16]) for l in range(32)]),
             reads=["W1", "kcraw_m", "kcraw_h"], writes=["PJ%d" % b])
        P.op(ACT, lambda e, b=b: e.activation(out=zt[:, 0, :], in_=PJ[b][:, 0:32], func=AF.Identity, bias=cbias[:, 0:1]), reads=["PJ%d" % b, "cbias"], writes=["zt0"])
        P.op(DVE, lambda e: e.tensor_tensor(out=zt[:, 1, :], in0=zt[:, 0, :], in1=zt[:, 0, :], op=ALU.mult), reads=["zt0"], writes=["zt1"])
        P.op(DVE, lambda e: e.tensor_scalar(out=zt[:, 1, :], in0=zt[:, 1, :], scalar1=0.044715, scalar2=1.0, op0=ALU.mult, op1=ALU.add), reads=["zt1"], writes=["zt1"])
        P.op(DVE, lambda e: e.tensor_tensor(out=zt[:, 1, :], in0=zt[:, 1, :], in1=zt[:, 0, :], op=ALU.mult), reads=["zt1", "zt0"], writes=["zt1"])
        P.op(ACT, lambda e: e.activation(out=zt[:, 2, :], in_=zt[:, 1, :], func=AF.Exp, scale=-2.0 * math.sqrt(2.0 / math.pi)), reads=["zt1"], writes=["zt2"])
        P.op(DVE, lambda e: e.tensor_scalar(out=zt[:, 2, :], in0=zt[:, 2, :], scalar1=1.0, scalar2=None, op0=ALU.add), reads=["zt2"], writes=["zt2"])
        P.op(DVE, lambda e: e.reciprocal(out=zt[:, 2, :], in_=zt[:, 2, :]), reads=["zt2"], writes=["zt2"])
        P.op(DVE, lambda e, jc=jc: e.tensor_tensor(out=geluT[:, jc * 32:(jc + 1) * 32], in0=zt[:, 0, :], in1=zt[:, 2, :], op=ALU.mult), reads=["zt0", "zt2", "geluT"], writes=["geluT"])
        b = pj()
        P.op(PE, lambda e, b=b, jc=jc: e.matmul(PJ[b][:, 0:32], W2[:, :], geluT[:, jc * 32:(jc + 1) * 32], start=True, stop=True), reads=["W2", "geluT"], writes=["PJ%d" % b])
        _copy(P, DVE, KcT[0:64, 32 * j:32 * j + 32], PJ[b][0:64, 0:32], ["PJ%d" % b, "KcT_all"], ["KcT_%d" % j])
        b = pj()
        P.op(PE, lambda e, b=b: e.matmul(PJ[b][:, 0:64], geluT[64:128, :], W2[64:128, 64:128], start=True, stop=True), reads=["W2", "geluT"], writes=["PJ%d" % b])
        _copy(P, DVE, Vcm[:, tcj, 0:64], PJ[b][:, 0:64], ["PJ%d" % b, "Vcm_all"], ["Vcm_%d" % tcj])
        if j == 0:
            P.op(DVE, lambda e: e.memset(Vcm[0:1, 0, :], 0.0), reads=["Vcm_ones"], writes=["Vcm_0", "Vcm_ones"])

        items = []
        for hl in range(4):
            obn[0] += 1
            ob = obn[0] % 2
            nch = tcj + 1
            for tc in range(nch):
                m = j - 4 * tc
                qk = [(KcT[0:67, tc * 128:(tc + 1) * 128], QTn[st][0:67, hl, :], 0, QT)]
                rd = ["KcT_all", "KcTal", "QT%d_%d" % (st, hl), "QTal%d" % st] + ["KcT_%d" % jj for jj in range(4 * tc, min(4 * tc + 4, j + 1))]
                if m <= 4:
                    qk.append((ident_b[:, :], cmask[:, m, :], 0, QT))
                    rd += ["ident", "cmask"]

                def pv(e, Pb, a_, b_, hl=hl, tc=tc, nch=nch, ob=ob):
                    e.matmul(O[ob][0:65, :], Vcm[:, tc, :], Pb[:, :], start=(tc == 0), stop=(tc == nch - 1), skip_group_check=True)
                    r = None
                    for u in range(4):
                        r = e.matmul(MS[:, u, :], Pb[:, u * 128:(u + 1) * 128], OV_sb[:, tc, :], start=(tc == 0 and u == 0), stop=(tc == nch - 1), skip_group_check=True)
                    return r
                bcol = 4 * 67 + hl * 16 + m
                items.append(dict(qk=qk, qk_reads=rd, c0=0, c1=QT, scale=scale, bias=btab[:, bcol:bcol + 1], exp_reads=["btab"], pv=pv,
                                  pv_reads=["Vcm_all", "Vcm_ones", "Vcm_0", "OV"] + ["Vcm_%d" % tt for tt in range(tcj + 1)], pv_writes=["O%d" % ob, "MS"]))

            def fin_c(hl=hl, ob=ob):
                P.op(ACT, lambda e: e.copy(out=ObC[:, hl, :], in_=O[ob][0:65, :]), reads=["O%d" % ob], writes=["ObC_%d" % hl])
                P.op(DVE, lambda e: e.tensor_reduce(out=lsum[:, :], in_=MS[:, :, :], axis=mybir.AxisListType.X, op=ALU.add), reads=["MS"], writes=["lsum"])
                P.op(DVE, lambda e: e.tensor_scalar(out=lsum[:, :], in0=lsum[:, :], scalar1=1e-30, scalar2=None, op0=ALU.max), reads=["lsum"], writes=["lsum"])
                P.op(DVE, lambda e: e.reciprocal(out=lsum[:, :], in_=lsum[:, :]), reads=["lsum"], writes=["lsum"])
                for u in range(4):
                    if hl == 0:
                        P.op(DVE, lambda e, u=u: e.tensor_scalar(out=impacc[:, u, :], in0=MS[:, u, :], scalar1=lsum[:, u:u + 1], scalar2=None, op0=ALU.mult),
                             reads=["MS", "lsum"], writes=["impacc%d" % u])
                    else:
                        P.op(DVE, lambda e, u=u: e.scalar_tensor_tensor(out=impacc[:, u, :], in0=MS[:, u, :], scalar=lsum[:, u:u + 1], in1=impacc[:, u, :],
                                                                        op0=ALU.mult, op1=ALU.add), reads=["MS", "lsum", "impacc%d" % u], writes=["impacc%d" % u])
            items[-1]["after"] = fin_c
        pipe.run(items)

        for u in range(4):
            qb = 4 * j + u
            off = 128 - 2 * qb
            P.op(DVE, lambda e, u=u, off=off: e.tensor_tensor(out=impw[:, 0, :], in0=impacc[:, u, :], in1=Tt[:, off:off + 128], op=ALU.add),
                 reads=["impacc%d" % u, "Tt"], writes=["impw0"])
            P.op(DVE, lambda e: e.tensor_scalar(out=impw[:, 0, 0:1], in0=impw[:, 0, 0:1], scalar1=1000.0, scalar2=None, op0=ALU.add), reads=["impw0"], writes=["impw0"])
            P.op(DVE, lambda e: e.max(out=m8[:, 0:8], in_=impw[:, 0, :]), reads=["impw0"], writes=["m8a"])
            P.op(DVE, lambda e: e.match_replace(out=impw[:, 1, :], in_to_replace=m8[:, 0:8], in_values=impw[:, 0, :], imm_value=-3.0e38), reads=["impw0", "m8a"], writes=["impw1"])
            P.op(DVE, lambda e: e.max(out=m8[:, 8:16], in_=impw[:, 1, :]), reads=["impw1"], writes=["m8b"])
            P.op(DVE, lambda e, u=u: e.tensor_scalar(out=negsel[:, u, :], in0=impw[:, 0, :], scalar1=m8[:, 15:16], scalar2=1.0, op0=ALU.is_ge, op1=ALU.subtract),
                 reads=["impw0", "m8b"], writes=["negsel%d" % u])
        b = pj()
        TPb = PJ[b][:, 0:256].bitcast(BF16)

        def tps(e, TPb=TPb):
            r = None
            for u in range(4):
                r = e.transpose(TPb[:, u * 128:(u + 1) * 128], negsel[:, u, :], ident_b[:, :])
            return r
        P.op(PE, tps, reads=["negsel%d" % u for u in range(4)] + ["ident"], writes=["PJ%d" % b])
        P.op(DVE, lambda e, TPb=TPb: e.tensor_copy(out=NST[:, :], in_=TPb), reads=["PJ%d" % b], writes=["NST"])

        for hl in range(4):
            items = []
            obn[0] += 1
            ob = obn[0] % 2
            nt = 4 * j + 4
            for t in range(nt):
                d = t - 4 * j
                a = 128 * d if d > 0 else 0
                qk = [(KT2[0:67, 0, t * 128:(t + 1) * 128], QTn[st][0:67, hl, a:QT], a, QT),
                      (E_sb[:, t * 128:(t + 1) * 128], NST[:, a:QT], a, QT)]
                rd = ["KTs_%d" % (t // 4), "KTal", "QT%d_%d" % (st, hl), "QTal%d" % st, "E", "NST"]
                if d >= 0:
                    qk.append((ident_b[:, :], negtri_b[:, :], a, a + 128))
                    rd += ["ident", "negtri"]

                def pv(e, Pb, a_, b_, t=t, nt=nt, ob=ob):
                    return e.matmul(O[ob][0:65, a_:b_], V2[:, t, 0, :], Pb[:, a_:b_], start=(t == 0), stop=(t == nt - 1), skip_group_check=True)
                bcol = hl * 67 + (4 * j - t) + 3
                items.append(dict(qk=qk, qk_reads=rd, c0=a, c1=QT, scale=scale, bias=btab[:, bcol:bcol + 1], exp_reads=["btab"], pv=pv,
                                  pv_reads=["V2_%d" % t, "V2_ones"], pv_writes=["O%d" % ob]))

            def fin_s(ob=ob):
                P.op(ACT, lambda e: e.copy(out=Ob3[:, 1, :], in_=O[ob][0:65, :]), reads=["O%d" % ob], writes=["Ob3_1"])
            items[-1]["after"] = fin_s
            obn[0] += 1
            ob2 = obn[0] % 2
            t_lo = max(0, 4 * j - 4)
            for t in range(t_lo, nt):
                d = t - 4 * j
                if d < 0:
                    m = t - (4 * j - 4)
                    a, bcol = 0, 128 * (m + 1)
                    mk = (ident_b[:, :], neganti_b[:, :], 128 * m, 128 * m + 128)
                    mkn = "neganti"
                else:
                    a, bcol = 128 * d, QT
                    mk = (ident_b[:, :], negtri_b[:, :], a, a + 128)
                    mkn = "negtri"
                qk = [(KT2[0:67, 1, t * 128:(t + 1) * 128], QTn[st][0:67, hl, a:bcol], a, bcol), mk]
                rd = ["KTw_%d" % (t // 4), "KTal", "QT%d_%d" % (st, hl), "QTal%d" % st, "ident", mkn]

                def pvw(e, Pb, a_, b_, t=t, nt=nt, ob2=ob2, t_lo=t_lo):
                    return e.matmul(O[ob2][0:65, a_:b_], V2[:, t, 1, :], Pb[:, a_:b_], start=(t == t_lo), stop=(t == nt - 1), skip_group_check=True)
                bc2 = hl * 67 + (4 * j - t) + 3
                items.append(dict(qk=qk, qk_reads=rd, c0=a, c1=bcol, scale=scale, bias=btab[:, bc2:bc2 + 1], exp_reads=["btab"], pv=pvw,
                                  pv_reads=["V2_%d" % t, "V2_ones"], pv_writes=["O%d" % ob2]))

            def fin_w(hl=hl, ob2=ob2, c0=c0):
                P.op(ACT, lambda e: e.copy(out=Ob3[:, 2, :], in_=O[ob2][0:65, :]), reads=["O%d" % ob2], writes=["Ob3_2"])
                srcs = [(ObC, hl, "ObC_%d" % hl), (Ob3, 1, "Ob3_1"), (Ob3, 2, "Ob3_2")]
                for bi, (src, idx, skey) in enumerate(srcs):
                    P.op(DVE, lambda e, src=src, idx=idx, bi=bi: e.tensor_scalar(out=rl3[64:65, bi, :], in0=src[64:65, idx, :], scalar1=1e-30, scalar2=None, op0=ALU.max),
                         reads=[skey], writes=["rl3_%d" % bi])
                    P.op(DVE, lambda e, bi=bi: e.reciprocal(out=rl3[64:65, bi, :], in_=rl3[64:65, bi, :]), reads=["rl3_%d" % bi], writes=["rl3_%d" % bi])
                    b = pj()
                    P.op(PE, lambda e, b=b, bi=bi: e.matmul(PJ[b][0:64, :], ones_f[64:65, 0:64], rl3[64:65, bi, :], start=True, stop=True),
                         reads=["ones_f", "rl3_%d" % bi], writes=["PJ%d" % b])
                    P.op(DVE, lambda e, b=b, bi=bi, src=src, idx=idx: e.tensor_tensor(out=tmpc[:, bi, :], in0=src[0:64, idx, :], in1=PJ[b][0:64, :], op=ALU.mult),
                         reads=[skey, "PJ%d" % b], writes=["tmpc%d" % bi])
                    b = pj()
                    P.op(PE, lambda e, b=b, bi=bi: e.matmul(PJ[b][0:64, :], gsel[0:12, hl * 3 + bi, :], gT[0:12, :], start=True, stop=True),
                         reads=["gsel", "gT"], writes=["PJ%d" % b])
                    P.op(DVE, lambda e, b=b, bi=bi: e.tensor_tensor(out=tmpc[:, bi, :], in0=tmpc[:, bi, :], in1=PJ[b][0:64, :], op=ALU.mult),
                         reads=["tmpc%d" % bi, "PJ%d" % b], writes=["tmpc%d" % bi])
                P.op(POOL, lambda e: e.tensor_tensor(out=tmpc[:, 0, :], in0=tmpc[:, 0, :], in1=tmpc[:, 1, :], op=ALU.add), reads=["tmpc0", "tmpc1"], writes=["tmpc0"])
                ostn[0] += 1
                os_ = ostn[0] % 2
                P.op(POOL, lambda e: e.tensor_tensor(out=ost[os_][:], in0=tmpc[:, 0, :], in1=tmpc[:, 2, :], op=ALU.add), reads=["tmpc0", "tmpc2"], writes=["ost%d" % os_])
                P.op(SP, lambda e: e.dma_start(out=oT[hl * 64:(hl + 1) * 64, c0:c0 + QT], in_=ost[os_][:]), reads=["ost%d" % os_], dma_key="ost%d" % os_)
            items[-1]["after"] = fin_w
            pipe.run(items)
    P.emit()
    return kb


def prep_nsa(inp, j, S):
    w_in = np.asarray(inp['l0_w_in'])
    base = 544
    nq = w_in[:, base:base + 512].reshape(D, 8, 64)[:, 4 * j:4 * j + 4].reshape(D, 256)
    seg = lambda i: w_in[:, base + 512 + 128 * i + 64 * j: base + 512 + 128 * i + 64 * j + 64]
    ng = w_in[:, base + 512 + 768:base + 512 + 768 + 24].reshape(D, 8, 3)[:, 4 * j:4 * j + 4].reshape(D, 12)
    wb = np.concatenate([nq, seg(0), seg(1), seg(2), seg(4), seg(3), seg(5), ng], 1)
    w1k, w1v = np.asarray(inp['l0_nsa_cmp_w1_k']), np.asarray(inp['l0_nsa_cmp_w1_v'])
    W1 = np.zeros((128, 32, 128), np.float32)
    W1[0:64, :, 0:64] = w1k.reshape(32, 64, 64).transpose(1, 0, 2)
    W1[64:128, :, 64:128] = w1v.reshape(32, 64, 64).transpose(1, 0, 2)
    W2 = np.zeros((128, 128), np.float32)
    W2[0:64, 0:64] = np.asarray(inp['l0_nsa_cmp_w2_k'])
    W2[64:128, 64:128] = np.asarray(inp['l0_nsa_cmp_w2_v'])
    posT = np.concatenate([np.asarray(inp['l0_nsa_cmp_pos_k']).T, np.asarray(inp['l0_nsa_cmp_pos_v']).T], 0)
    slopes = _alibi_slopes(8)[4 * j:4 * j + 4]
    NSLOT = S // 16
    NCH = (NSLOT + 127) // 128
    si = np.arange(NCH * 128) % 128
    kalc = np.stack([np.ones(NCH * 128), np.ones(NCH * 128), 16.0 * si])
    kal = np.stack([np.ones(S), np.ones(S), np.arange(S) % 128])
    cc = _consts_common()
    sl = np.arange(128)[:, None]
    qi = np.arange(QT)[None, :]
    cmask = np.stack([np.where(512 * m + qi - 16 * sl - 15 >= 0, 0.0, NEG) for m in range(5)], 1).reshape(128, 5 * QT)
    E = np.zeros((128, S), np.float32)
    E[np.arange(S) // 64, np.arange(S)] = -NEG
    s_all = np.arange(NCH * 128)
    cstart = 16 * (s_all - 1)
    n = np.arange(128)
    ov = np.minimum(cstart[:, None] + 32, 64 * n[None, :] + 64) - np.maximum(cstart[:, None], 64 * n[None, :])
    ov = np.maximum(ov, 0).astype(np.float32) / 32.0
    ov[0] = 0.0
    ov[s_all >= NSLOT] = 0.0
    OV = ov.reshape(NCH, 128, 128).transpose(1, 0, 2).reshape(128, NCH * 128)
    r = np.arange(-128, 128)[None, :]
    cur = (np.arange(128)[:, None] >= 64).astype(np.int64)
    Tt = np.where(r > cur, -1e30, np.where((r == cur) | (r == cur - 1), 1000.0, 0.0))
    gsel = np.zeros((12, 12, 64), np.float32)
    for a in range(12):
        gsel[a, a, :] = 1.0
    d = dict(btab=_btab(slopes, True), wb=wb, W1bd=W1.reshape(128, 4096), W2bd=W2, posT=posT, qal=_qal_rows(slopes, 0.125), kal=kal, kalc=kalc, ident=cc['ident'],
             negtri=cc['negtri'], neganti=cc['neganti'], cmask=cmask, E=E, OV=OV, Tt=Tt, gsel=gsel.reshape(12, 768))
    return {k: np.ascontiguousarray(v, dtype=np.float32) for k, v in d.items()}


def _btab(slopes4, with_cmp):
    cols = []
    for sl in slopes4:
        cols += [-float(sl) * 128.0 * dd for dd in range(-3, 64)]
    if with_cmp:
        for sl in slopes4:
            cols += [-float(sl) * (512.0 * m - 15.0) for m in range(16)]
    return np.tile(np.asarray(cols, np.float32)[None, :], (128, 1))


def prep_diff(inp, j, S):
    wqkv = np.asarray(inp['l1_w_qkv'])
    slopes = _alibi_slopes(8)[4 * j:4 * j + 4]
    cc = _consts_common()
    d = dict(wq=wqkv[:, 512 * j:512 * j + 512], wk=wqkv[:, 1024 + 512 * j:1024 + 512 * j + 512], wv=wqkv[:, 2048 + 512 * j:2048 + 512 * j + 512],
             lamv=np.concatenate([np.asarray(inp[k]) for k in ('l1_lam_q1', 'l1_lam_k1', 'l1_lam_q2', 'l1_lam_k2')])[None],
             gsub=np.asarray(inp['l1_subln_g'])[:, None], qal=_qal_rows(slopes, 0.125),
             kal=np.stack([np.ones(S), np.ones(S), np.arange(S) % 128]), ident=cc['ident'], negtri=cc['negtri'], btab=_btab(slopes, False))
    return {k: np.ascontiguousarray(v, dtype=np.float32) for k, v in d.items()}


_RUN = [None]
_PROGS = {}


def _launch(name, builder, in_maps):
    if name not in _PROGS:
        _PROGS[name] = builder()
    kb = _PROGS[name]
    runner = _RUN[0] or (lambda nc, im: run_bass_kernel_spmd(nc, im, core_ids=list(range(len(im)))).results)
    return runner(kb.nc, in_maps)


def kernel_impl(inputs, B, S):
    inp = {k: np.asarray(v) for k, v in inputs.items()}
    x = inp['x']
    NC = 2 * B
    H = S // 2
    lam_init = 0.8 - 0.6 * math.exp(-0.3 * 1)
    cores = [(c // 2, c % 2) for c in range(NC)]
    xTs = [np.ascontiguousarray(x[b].T) for b in range(B)]
    pm = [prep_mla(inp, j, S) for j in range(2)]
    pn = [prep_nsa(inp, j, S) for j in range(2)]
    r_mla = _launch("mla%d" % S, lambda: build_mla(S), [dict(pm[j], xT=xTs[b]) for b, j in cores])
    r_nsa = _launch("nsa%d" % S, lambda: build_nsa(S), [dict(pn[j], xT=xTs[b]) for b, j in cores])
    o0T = [np.concatenate([r_mla[2 * b]['oT'], r_mla[2 * b + 1]['oT'], r_nsa[2 * b]['oT'], r_nsa[2 * b + 1]['oT']], 0) for b in range(B)]
    ident = np.eye(128, dtype=np.float32)

    def post(oTs, xs, pfx):
        ln = np.ascontiguousarray(np.stack([inp[pfx + '_ln_mix_g'], inp[pfx + '_ln_mix_b'], inp[pfx + '_ln_ffn_g'], inp[pfx + '_ln_ffn_b']]), dtype=np.float32)
        wo = inp['l0_w_out'] if pfx == 'l0' else inp['l1_w_o']
        ims = [dict(oT=np.ascontiguousarray(oTs[b][:, H * j:H * (j + 1)]), x=np.ascontiguousarray(xs[b][H * j:H * (j + 1)]), w_out=wo,
                    w_up=inp[pfx + '_w_up'], w_down=inp[pfx + '_w_down'], ln=ln, ident=ident) for b, j in cores]
        r = _launch("post%d" % H, lambda: build_post(H, True), ims)
        ys = [np.concatenate([r[2 * b]['y'], r[2 * b + 1]['y']], 0) for b in range(B)]
        yTs = [np.concatenate([r[2 * b]['yT'], r[2 * b + 1]['yT']], 1) for b in range(B)]
        return ys, yTs
    x1, x1T = post(o0T, [x[b] for b in range(B)], 'l0')
    pd = [prep_diff(inp, j, S) for j in range(2)]
    r_d = _launch("diff%d" % S, lambda: build_diff(S, lam_init), [dict(pd[j], xT=np.ascontiguousarray(x1T[b])) for b, j in cores])
    o1T = [np.concatenate([r_d[2 * b]['oT'], r_d[2 * b + 1]['oT']], 0) for b in range(B)]
    x2, _ = post(o1T, x1, 'l1')
    return np.stack(x2, 0).astype(np.float32)


def kernel(**inputs):
    return kernel_impl(inputs, 4, 8192)
```
